# Optimizing a Trainium2 kernel written in Bass

```python
import math
import jax, jax.numpy as jnp
from jax import lax
import numpy as np

D_MODEL = 4096
BATCH = 32
SEQ = 256
DEPTH = 2
DEC_BATCH = 8
DEC_SEQ = 1024
PAST_LEN = 256

GRID_W = 64
Q_BLOCK = 128
ROPE_THETA = 10000.0
NORM_EPS = 1e-6
D_FF = 11008
N_MOD = 9
DIFF_HEADS = 8
DIFF_QK = 64
DIFF_V = 128
HEAD_DIM = 128
GQA_HEADS = 12
GQA_KV_HEADS = 4
GQA_GROUP = GQA_HEADS // GQA_KV_HEADS
MLA_HEADS = 12
MLA_NOPE = 128
MLA_ROPE = 64
MLA_V = 128
Q_LORA = 1024
KV_LORA = 512
MIX_WIDTH = DIFF_HEADS * DIFF_V + GQA_HEADS * HEAD_DIM + MLA_HEADS * MLA_V
IN_SIZES = (DIFF_HEADS * 2 * DIFF_QK, DIFF_HEADS * 2 * DIFF_QK, DIFF_HEADS * DIFF_V,
            GQA_HEADS * HEAD_DIM, GQA_KV_HEADS * HEAD_DIM, GQA_KV_HEADS * HEAD_DIM,
            Q_LORA, KV_LORA, MLA_ROPE)
IN_COLS = sum(IN_SIZES)
IN_SPLITS = tuple(int(s) for s in np.cumsum(IN_SIZES)[:-1])

kernel_name = "hymba_diff_gqa_mla_prefix_dit_step"


def rmsnorm(x, g):
    xf = x.astype(jnp.float32)
    y = xf * lax.rsqrt(jnp.mean(xf * xf, axis=-1, keepdims=True) + NORM_EPS)
    return (y * g.astype(jnp.float32)).astype(x.dtype)


def modulate(x, shift, scale):
    return x * (1 + scale) + shift


def swiglu(u, w_in, w_out):
    gate, up = jnp.split(u @ w_in, 2, axis=-1)
    return (jax.nn.silu(gate) * up) @ w_out


def rope_1d(x, pos):
    d = x.shape[-1]
    inv = ROPE_THETA ** (-(jnp.arange(0, d, 2, dtype=jnp.float32) / d))
    ang = pos[:, None] * inv[None, :]
    shape = (ang.shape[0],) + (1,) * (x.ndim - 3) + (ang.shape[1],)
    cos = jnp.cos(ang).reshape(shape)
    sin = jnp.sin(ang).reshape(shape)
    x1, x2 = jnp.split(x, 2, axis=-1)
    return jnp.concatenate([x1 * cos - x2 * sin, x1 * sin + x2 * cos], axis=-1)


def axial_rope(x, pos):
    rows, cols = pos
    half = x.shape[-1] // 2
    xf = x.astype(jnp.float32)
    out = jnp.concatenate([rope_1d(xf[..., :half], rows), rope_1d(xf[..., half:], cols)], axis=-1)
    return out.astype(x.dtype)


def sweep_query_blocks(fn, q):
    B, T = q.shape[:2]
    qb = q.reshape((B, T // Q_BLOCK, Q_BLOCK) + q.shape[2:])
    out = lax.map(fn, jnp.moveaxis(qb, 1, 0))
    out = jnp.moveaxis(out, 0, 1)
    return out.reshape((B, T) + out.shape[3:])


def softmax_attention(q, k, v):
    scale = q.shape[-1] ** -0.5

    def block(qb):
        s = jnp.einsum("bqhgd,bshd->bhgqs", qb, k, preferred_element_type=jnp.float32) * scale
        p = jax.nn.softmax(s, axis=-1).astype(v.dtype)
        return jnp.einsum("bhgqs,bshe->bqhge", p, v)

    return sweep_query_blocks(block, q)


def diff_attention(q, k, v, lam):
    scale = q.shape[-1] ** -0.5

    def block(qb):
        s = jnp.einsum("bqhcd,bshcd->bhcqs", qb, k, preferred_element_type=jnp.float32) * scale
        p = jax.nn.softmax(s, axis=-1)
        a = (p[:, :, 0] - lam * p[:, :, 1]).astype(v.dtype)
        return jnp.einsum("bhqs,bshe->bqhe", a, v)

    return sweep_query_blocks(block, q)


def project_mixers(u, lp, pos):
    B, T, _ = u.shape
    dq, dk, dv, gq, gk, gv, cq, ckv, kpe = jnp.split(u @ lp["w_in"], IN_SPLITS, axis=-1)
    dq = dq.reshape(B, T, DIFF_HEADS, 2, DIFF_QK)
    dk = dk.reshape(B, T, DIFF_HEADS, 2, DIFF_QK)
    dv = dv.reshape(B, T, DIFF_HEADS, DIFF_V)
    gq = rmsnorm(gq.reshape(B, T, GQA_HEADS, HEAD_DIM), lp["gqa_q_norm"])
    gk = rmsnorm(gk.reshape(B, T, GQA_KV_HEADS, HEAD_DIM), lp["gqa_k_norm"])
    gv = gv.reshape(B, T, GQA_KV_HEADS, HEAD_DIM)
    mq = (rmsnorm(cq, lp["mla_q_norm"]) @ lp["mla_w_uq"]).reshape(B, T, MLA_HEADS, MLA_NOPE + MLA_ROPE)
    mq_nope, mq_pe = mq[..., :MLA_NOPE], mq[..., MLA_NOPE:]
    ckv = rmsnorm(ckv, lp["mla_kv_norm"])
    if pos is not None:
        dq = axial_rope(dq, pos)
        dk = axial_rope(dk, pos)
        gq = axial_rope(gq, pos)
        gk = axial_rope(gk, pos)
        mq_pe = axial_rope(mq_pe, pos)
        kpe = axial_rope(kpe, pos)
    mq = jnp.concatenate([mq_nope, mq_pe], axis=-1)
    gq = gq.reshape(B, T, GQA_KV_HEADS, GQA_GROUP, HEAD_DIM)
    return (dq, gq, mq), (dk, dv, gk, gv, ckv, kpe)


def attend_mixers(queries, keys, lp, l):
    dq, gq, mq = queries
    dk, dv, gk, gv, ckv, kpe = keys
    B, T = dq.shape[:2]
    S = dk.shape[1]
    lam_init = 0.8 - 0.6 * math.exp(-0.3 * l)
    f32 = jnp.float32
    lam = (jnp.exp(jnp.sum(lp["diff_lq1"].astype(f32) * lp["diff_lk1"].astype(f32)))
           - jnp.exp(jnp.sum(lp["diff_lq2"].astype(f32) * lp["diff_lk2"].astype(f32))) + lam_init)
    o_diff = rmsnorm(diff_attention(dq, dk, dv, lam), lp["diff_subln"]) * (1.0 - lam_init)
    o_gqa = softmax_attention(gq, gk, gv)
    kv = (ckv @ lp["mla_w_ukv"]).reshape(B, S, MLA_HEADS, MLA_NOPE + MLA_V)
    k_nope, v_mla = kv[..., :MLA_NOPE], kv[..., MLA_NOPE:]
    k_mla = jnp.concatenate([k_nope, jnp.broadcast_to(kpe[:, :, None, :], (B, S, MLA_HEADS, MLA_ROPE))], axis=-1)
    o_mla = softmax_attention(mq[:, :, :, None, :], k_mla, v_mla)
    out = jnp.concatenate([o_diff.reshape(B, T, -1), o_gqa.reshape(B, T, -1), o_mla.reshape(B, T, -1)], axis=-1)
    return out @ lp["w_out"]


def trunk_layer(x, cvec, lp, l, pos, ctx):
    mod = jax.nn.silu(cvec) @ lp["w_mod"] + lp["b_mod"]
    sh1, sc1, g1, sh2, sc2, g2, sh3, sc3, g3 = jnp.split(mod[:, None, :], N_MOD, axis=-1)
    x = x + 0.5 * g1 * swiglu(modulate(rmsnorm(x, lp["norm_ffn1"]), sh1, sc1), lp["ffn1_w_in"], lp["ffn1_w_out"])
    u = modulate(rmsnorm(x, lp["norm_mix"]), sh2, sc2)
    queries, own = project_mixers(u, lp, pos)
    if ctx is None:
        keys = own
    else:
        keys = tuple(jnp.concatenate([kc, ko], axis=1) for kc, ko in zip(ctx, own))
    x = x + g2 * attend_mixers(queries, keys, lp, l)
    x = x + 0.5 * g3 * swiglu(modulate(rmsnorm(x, lp["norm_ffn2"]), sh3, sc3), lp["ffn2_w_in"], lp["ffn2_w_out"])
    return x, own


def setup_inputs(seed: int = 0) -> dict:
    key = jax.random.key(seed)
    ks = iter(jax.random.split(key, 40))
    D = D_MODEL

    def nrm(shape, scale):
        return jax.random.normal(next(ks), shape, jnp.float32) * scale

    def gain(shape):
        return 1.0 + nrm(shape, 0.05)

    return {
        "x_prompt": nrm((BATCH, SEQ, D), 1.0),
        "x_sample": nrm((DEC_BATCH, DEC_SEQ, D), 1.0),
        "cache_diff_k": nrm((DEC_BATCH, DEPTH, PAST_LEN, DIFF_HEADS, 2, DIFF_QK), 1.0),
        "cache_diff_v": nrm((DEC_BATCH, DEPTH, PAST_LEN, DIFF_HEADS, DIFF_V), 1.0),
        "cache_gqa_k": nrm((DEC_BATCH, DEPTH, PAST_LEN, GQA_KV_HEADS, HEAD_DIM), 1.0),
        "cache_gqa_v": nrm((DEC_BATCH, DEPTH, PAST_LEN, GQA_KV_HEADS, HEAD_DIM), 1.0),
        "cache_mla_ckv": nrm((DEC_BATCH, DEPTH, PAST_LEN, KV_LORA), 1.0),
        "cache_mla_kpe": nrm((DEC_BATCH, DEPTH, PAST_LEN, MLA_ROPE), 1.0),
        "c": nrm((DEC_BATCH, D), 1.0),
        "c_ctx": nrm((D,), 1.0),
        "w_mod": nrm((DEPTH, D, N_MOD * D), 0.5 * D ** -0.5),
        "b_mod": nrm((DEPTH, N_MOD * D), 0.02),
        "norm_ffn1": gain((DEPTH, D)),
        "norm_mix": gain((DEPTH, D)),
        "norm_ffn2": gain((DEPTH, D)),
        "ffn1_w_in": nrm((DEPTH, D, 2 * D_FF), D ** -0.5),
        "ffn1_w_out": nrm((DEPTH, D_FF, D), D_FF ** -0.5),
        "ffn2_w_in": nrm((DEPTH, D, 2 * D_FF), D ** -0.5),
        "ffn2_w_out": nrm((DEPTH, D_FF, D), D_FF ** -0.5),
        "w_in": nrm((DEPTH, D, IN_COLS), D ** -0.5),
        "w_out": nrm((DEPTH, MIX_WIDTH, D), MIX_WIDTH ** -0.5),
        "diff_lq1": nrm((DEPTH, DIFF_QK), 0.1),
        "diff_lk1": nrm((DEPTH, DIFF_QK), 0.1),
        "diff_lq2": nrm((DEPTH, DIFF_QK), 0.1),
        "diff_lk2": nrm((DEPTH, DIFF_QK), 0.1),
        "diff_subln": gain((DEPTH, DIFF_V)),
        "gqa_q_norm": gain((DEPTH, HEAD_DIM)),
        "gqa_k_norm": gain((DEPTH, HEAD_DIM)),
        "mla_q_norm": gain((DEPTH, Q_LORA)),
        "mla_kv_norm": gain((DEPTH, KV_LORA)),
        "mla_w_uq": nrm((DEPTH, Q_LORA, MLA_HEADS * (MLA_NOPE + MLA_ROPE)), Q_LORA ** -0.5),
        "mla_w_ukv": nrm((DEPTH, KV_LORA, MLA_HEADS * (MLA_NOPE + MLA_V)), KV_LORA ** -0.5),
        "final_norm": gain((D,)),
    }


def reference(x_prompt, x_sample, cache_diff_k, cache_diff_v, cache_gqa_k, cache_gqa_v, cache_mla_ckv,
              cache_mla_kpe, c, c_ctx, w_mod, b_mod, norm_ffn1, norm_mix, norm_ffn2, ffn1_w_in, ffn1_w_out,
              ffn2_w_in, ffn2_w_out, w_in, w_out, diff_lq1, diff_lk1, diff_lq2, diff_lk2, diff_subln,
              gqa_q_norm, gqa_k_norm, mla_q_norm, mla_kv_norm, mla_w_uq, mla_w_ukv, final_norm):
    def layer_params(l):
        return {
            "w_mod": w_mod[l], "b_mod": b_mod[l],
            "norm_ffn1": norm_ffn1[l], "norm_mix": norm_mix[l], "norm_ffn2": norm_ffn2[l],
            "ffn1_w_in": ffn1_w_in[l], "ffn1_w_out": ffn1_w_out[l],
            "ffn2_w_in": ffn2_w_in[l], "ffn2_w_out": ffn2_w_out[l],
            "w_in": w_in[l], "w_out": w_out[l],
            "diff_lq1": diff_lq1[l], "diff_lk1": diff_lk1[l], "diff_lq2": diff_lq2[l], "diff_lk2": diff_lk2[l],
            "diff_subln": diff_subln[l], "gqa_q_norm": gqa_q_norm[l], "gqa_k_norm": gqa_k_norm[l],
            "mla_q_norm": mla_q_norm[l], "mla_kv_norm": mla_kv_norm[l],
            "mla_w_uq": mla_w_uq[l], "mla_w_ukv": mla_w_ukv[l],
        }

    h = x_prompt
    ctx_states = [[] for _ in range(6)]
    for l in range(DEPTH):
        h, own = trunk_layer(h, c_ctx[None, :], layer_params(l), l, None, None)
        for i in range(6):
            ctx_states[i].append(own[i])
    y_prompt = rmsnorm(h, final_norm)

    T = x_sample.shape[1]
    ROWS = T // GRID_W
    rows = jnp.repeat(jnp.arange(ROWS, dtype=jnp.float32), GRID_W)
    cols = jnp.tile(jnp.arange(GRID_W, dtype=jnp.float32), ROWS)
    pos = (rows, cols)
    h = x_sample
    for l in range(DEPTH):
        ctx = (cache_diff_k[:, l], cache_diff_v[:, l], cache_gqa_k[:, l], cache_gqa_v[:, l],
               cache_mla_ckv[:, l], cache_mla_kpe[:, l])
        h, _ = trunk_layer(h, c, layer_params(l), l, pos, ctx)
    y_sample = rmsnorm(h, final_norm)

    new_diff_k = jnp.stack(ctx_states[0], axis=1)
    new_diff_v = jnp.stack(ctx_states[1], axis=1)
    new_gqa_k = jnp.stack(ctx_states[2], axis=1)
    new_gqa_v = jnp.stack(ctx_states[3], axis=1)
    new_mla_ckv = jnp.stack(ctx_states[4], axis=1)
    new_mla_kpe = jnp.stack(ctx_states[5], axis=1)
    return (y_prompt, y_sample, new_diff_k, new_diff_v, new_gqa_k, new_gqa_v, new_mla_ckv, new_mla_kpe)
```

```python
import math
import contextlib
import numpy as np
import concourse.bass as bass
import concourse.mybir as mybir
from concourse.bass_utils import run_bass_kernel_spmd

F32 = mybir.dt.float32
BF16 = mybir.dt.bfloat16
AF = mybir.ActivationFunctionType
ALU = mybir.AluOpType
AX = mybir.AxisListType

D = 4096
KC = 32
DFF = 11008
FC = 86
NT = 2048
NTILE = 16
NMOD = 9
INC = 7232
EPS = 1e-6
NCORES = 8
QW = 4864
KW = 3136
VW = 3072
KVROWS = 2304
ENGS = ["sync", "scalar", "vector", "gpsimd", "tensor"]
COMPUTE = ["scalar", "vector", "gpsimd", "tensor"]


class Res:
    __slots__ = ("name", "w", "r", "dsem")

    def __init__(self, name):
        self.name = name
        self.w = None
        self.r = []
        self.dsem = None


class SemRec:
    __slots__ = ("h", "count")

    def __init__(self, h):
        self.h = h
        self.count = 0


class Sched:
    def __init__(self, nc, es, n_dma_sems=90):
        self.nc = nc
        self.q = {e: [] for e in ENGS}
        self.esem = {e: SemRec(es.enter_context(nc.semaphore("tk_" + e))) for e in COMPUTE}
        self.seen = {e: {} for e in ENGS}
        self.pool = [SemRec(es.enter_context(nc.semaphore("dm%d" % i))) for i in range(n_dma_sems)]
        self.all_dma = list(self.pool)
        self.uid = 0

    def _deps(self, reads, writes):
        deps = []
        for r in reads:
            if r.w is not None:
                deps.append(r.w)
        for w in writes:
            if w.w is not None:
                deps.append(w.w)
            deps.extend(w.r)
        return deps

    def _waits(self, eng, deps):
        need = {}
        seen = self.seen[eng]
        for d in deps:
            if d[0] == "e":
                _, pe, val = d
                if pe == "tensor" and eng == "tensor":
                    continue
                sr = self.esem[pe]
            else:
                sr = d[1]
                val = sr.count
            k = id(sr)
            if val <= seen.get(k, 0):
                continue
            if k not in need or need[k][1] < val:
                need[k] = (sr, val)
        out = []
        for k, (sr, val) in need.items():
            seen[k] = val
            out.append((sr.h, val))
        return out

    def _mark(self, dep, reads, writes):
        for r in reads:
            r.r.append(dep)
        for w in writes:
            w.w = dep
            w.r = []

    def op(self, eng, fn, reads=(), writes=()):
        waits = self._waits(eng, self._deps(reads, writes))
        sr = self.esem[eng]
        sr.count += 1
        self.q[eng].append((fn, waits, (sr.h, 1)))
        self._mark(("e", eng, sr.count), reads, writes)

    def pe(self, fns, reads=(), writes=()):
        waits = self._waits("tensor", self._deps(reads, writes))
        sr = self.esem["tensor"]
        sr.count += 1
        n = len(fns)
        for i, fn in enumerate(fns):
            self.q["tensor"].append((fn, waits if i == 0 else (), (sr.h, 1) if i == n - 1 else None))
        self._mark(("e", "tensor", sr.count), reads, writes)

    def dma(self, queue, out, in_, reads=(), writes=(), **kw):
        owner = writes[0] if writes else reads[0]
        if owner.dsem is None:
            owner.dsem = self.pool.pop()
        sr = owner.dsem
        waits = self._waits(queue, self._deps(reads, writes))
        sr.count += 16
        self.q[queue].append((lambda e: e.dma_start(out=out, in_=in_, **kw), waits, (sr.h, 16)))
        self._mark(("d", sr), reads, writes)

    def release(self, res_list):
        for r in res_list:
            if r.dsem is not None:
                self.pool.append(r.dsem)
                r.dsem = None

    def barrier(self):
        for eng in ENGS:
            waits = []
            seen = self.seen[eng]
            for pe in COMPUTE:
                sr = self.esem[pe]
                if sr.count > seen.get(id(sr), 0):
                    seen[id(sr)] = sr.count
                    waits.append((sr.h, sr.count))
            for sr in self.all_dma:
                if sr.count > seen.get(id(sr), 0):
                    seen[id(sr)] = sr.count
                    waits.append((sr.h, sr.count))
            if waits:
                self.q[eng].append((None, waits, None))

    def emit(self):
        nc = self.nc
        with nc.Block() as block:
            for eng in ENGS:
                ops = self.q[eng]
                if not ops:
                    continue

                def body(e, ops=ops):
                    for fn, waits, inc in ops:
                        for (sem, val) in waits:
                            e.wait_ge(sem, val)
                        if fn is None:
                            continue
                        ins = fn(e)
                        if inc is not None:
                            ins.then_inc(inc[0], inc[1])

                getattr(block, eng)(body)


class Phase:
    def __init__(self, S):
        self.S = S
        self.es = contextlib.ExitStack()
        self.res = []

    def __enter__(self):
        self.es.__enter__()
        return self

    def sb(self, name, shape, dt=F32, nres=1):
        self.S.uid += 1
        t = self.es.enter_context(self.S.nc.sbuf_tensor("%s_%d" % (name, self.S.uid), shape, dt))
        rs = [Res(name + str(i)) for i in range(nres)]
        self.res.extend(rs)
        return (t, rs[0]) if nres == 1 else (t, rs)

    def ps(self, name, shape, dt=F32):
        self.S.uid += 1
        t = self.es.enter_context(self.S.nc.psum_tensor("%s_%d" % (name, self.S.uid), shape, dt))
        r = Res(name)
        self.res.append(r)
        return t, r

    def __exit__(self, *a):
        self.S.barrier()
        self.S.release(self.res)
        return self.es.__exit__(*a)


def lam_init(l):
    return 0.8 - 0.6 * math.exp(-0.3 * l)


def build_program(depth=2, debug=None):
    nc = bass.Bass("TRN2", target_bir_lowering=False)
    dt = nc.dram_tensor

    def din(name, shape, dtype=F32):
        return dt(name, shape, dtype, kind="ExternalInput").ap()

    def dout(name, shape, dtype=F32):
        return dt(name, shape, dtype, kind="ExternalOutput").ap()

    xin = [din("xp", [1024, D]), din("xs", [1024, D])]
    cdk = din("cdk", [2, 256, 1024]); cdv = din("cdv", [2, 256, 1024])
    cgk = din("cgk", [2, 256, 512]); cgv = din("cgv", [2, 256, 512])
    cckv = din("cckv", [2, 256, 512]); ckpe = din("ckpe", [2, 256, 64])
    cvec = din("cvec", [2, D])
    w_mod = din("w_mod", [2, D, NMOD * D]); b_mod = din("b_mod", [2, NMOD * D])
    norm_ffn1 = din("norm_ffn1", [2, D]); norm_mix = din("norm_mix", [2, D]); norm_ffn2 = din("norm_ffn2", [2, D])
    ffn_w_in = [din("ffn1_w_in", [2, D, 2 * DFF]), din("ffn2_w_in", [2, D, 2 * DFF])]
    ffn_w_out = [din("ffn1_w_out", [2, DFF, D]), din("ffn2_w_out", [2, DFF, D])]
    w_in = din("w_in", [2, D, INC]); w_out = din("w_out", [2, D, D])
    lq1 = din("diff_lq1", [2, 64]); lk1 = din("diff_lk1", [2, 64])
    lq2 = din("diff_lq2", [2, 64]); lk2 = din("diff_lk2", [2, 64])
    subln = din("diff_subln", [2, 128])
    gqn = din("gqa_q_norm", [2, 128]); gkn = din("gqa_k_norm", [2, 128])
    mqn = din("mla_q_norm", [2, 1024]); mkvn = din("mla_kv_norm", [2, 512])
    w_uq = din("mla_w_uq", [2, 1024, 2304]); w_ukv = din("mla_w_ukv", [2, 512, 3072])
    final_norm = din("final_norm", [D])
    ident_d = din("ident", [128, 128])
    cs64_d = din("cs64", [1024, 64]); cs128_d = din("cs128", [1024, 128])

    yout = [dout("yp", [1024, D]), dout("ys", [1024, D])]
    ndk = dout("ndk", [4, 2, 256, 1024]); ndv = dout("ndv", [4, 2, 256, 1024])
    ngk = dout("ngk", [4, 2, 256, 512]); ngv = dout("ngv", [4, 2, 256, 512])
    nckv = dout("nckv", [4, 2, 256, 512]); nkpe = dout("nkpe", [4, 2, 256, 64])

    XT = dt("XT", [D, NT], F32).ap()
    HT = dt("HT", [DFF, NT], BF16).ap()
    RAW = dt("RAW", [NT, INC], F32).ap()
    QT = dt("QT", [QW, NT], BF16).ap()
    KT = dt("KT", [KW, KVROWS], BF16).ap()
    VV = dt("VV", [KVROWS, VW], BF16).ap()
    dbg = None
    if debug:
        dbg = dout("dbg", [D, NT])

    XT3 = XT.rearrange("(kc p) t -> p kc t", p=128)

    with contextlib.ExitStack() as es:
        S = Sched(nc, es)
        with Phase(S) as G:
            ident, r_ident = G.sb("ident", [128, 128])
            identb, r_identb = G.sb("identb", [128, 128], BF16)
            onesb, r_onesb = G.sb("onesb", [128, 128], BF16)
            epsc, r_epsc = G.sb("epsc", [128, 1])
            modT, r_modT = G.sb("modT", [128, NMOD * KC, 2])
            coefA, r_coefA = G.sb("coefA", [128, 3, KC, 2])
            gT, r_gT = G.sb("gT", [128, 3, KC, 2])
            cv32, r_cv32 = G.sb("cv32", [32, 2, 128])
            cvT, r_cvT = G.sb("cvT", [128, KC, 2])
            scb, r_scb = G.sb("scb", [128, KC, 2], BF16)
            bmT, r_bmT = G.sb("bmT", [128, NMOD * KC])
            nrmT, r_nrmT = G.sb("nrmT", [128, 3, KC])
            S.dma("sync", ident[:], ident_d[:, :], writes=[r_ident])
            S.op("vector", lambda e: e.tensor_copy(identb[:], ident[:]), reads=[r_ident], writes=[r_identb])
            S.op("vector", lambda e: e.memset(onesb[:], 1.0), writes=[r_onesb])
            S.op("vector", lambda e: e.memset(epsc[:], EPS), writes=[r_epsc])

            def load_vec_T(ph, vec_ap, m, out_ap, out_res, pstile, psres):
                j0 = 0
                while j0 < m:
                    mm = min(96, m - j0)
                    tmp, r_tmp = ph.sb("lvt", [96, 128])
                    S.dma("sync", tmp[0:mm, :], vec_ap[j0 * 128:(j0 + mm) * 128].rearrange("(j p) -> j p", p=128),
                          writes=[r_tmp])
                    S.pe([lambda e, tmp=tmp, mm=mm: e.matmul(pstile[:, 0:mm], tmp[0:mm, :], ident[0:mm, 0:mm],
                                                             start=True, stop=True)],
                         reads=[r_tmp, r_ident], writes=[psres])
                    S.op("vector", lambda e, j0=j0, mm=mm: e.tensor_copy(out_ap[:, j0:j0 + mm], pstile[:, 0:mm]),
                         reads=[psres], writes=[out_res])
                    j0 += mm

            with Phase(S) as ph:
                pss = [ph.ps("ips%d" % i, [128, 512]) for i in range(4)]
                xts = [ph.sb("ixt%d" % i, [128, KC, 128]) for i in range(2)]
                xls = [ph.sb("ixl%d" % i, [128, D]) for i in range(2)]
                for tt in range(NTILE):
                    g, lt = tt // 8, tt % 8
                    xl, r_xl = xls[tt % 2]
                    xt, r_xt = xts[tt % 2]
                    S.dma("sync", xl[:], xin[g][lt * 128:(lt + 1) * 128, :], writes=[r_xl])
                    for k4 in range(8):
                        pst, r_ps = pss[k4 % 4]
                        S.pe([lambda e, pst=pst, xl=xl, kc=k4 * 4 + i, i=i: e.matmul(
                            pst[:, i * 128:(i + 1) * 128], xl[:, kc * 128:(kc + 1) * 128], ident[:],
                            start=True, stop=True) for i in range(4)],
                            reads=[r_xl, r_ident], writes=[r_ps])
                        eng = "vector" if k4 % 2 == 0 else "scalar"
                        if eng == "vector":
                            S.op("vector", lambda e, xt=xt, pst=pst, k4=k4: e.tensor_copy(
                                xt[:, k4 * 4:(k4 + 1) * 4, :], pst[:].rearrange("p (a b) -> p a b", a=4)),
                                reads=[r_ps], writes=[r_xt])
                        else:
                            S.op("scalar", lambda e, xt=xt, pst=pst, k4=k4: e.activation(
                                out=xt[:, k4 * 4:(k4 + 1) * 4, :], in_=pst[:].rearrange("p (a b) -> p a b", a=4),
                                func=AF.Copy), reads=[r_ps], writes=[r_xt])
                    S.dma("sync", XT3[:, :, tt * 128:(tt + 1) * 128], xt[:], reads=[r_xt])

            def mod_setup(l):
                with Phase(S) as ph:
                    psA, r_psA = ph.ps("mpsA", [128, 512])
                    for g in range(2):
                        S.dma("sync", cv32[:, g, :], cvec[g, :].rearrange("(j p) -> j p", p=128), writes=[r_cv32])
                    for g in range(2):
                        S.pe([lambda e, g=g: e.matmul(psA[:, g * 32:(g + 1) * 32], cv32[:, g, :], ident[0:32, 0:32],
                                                      start=True, stop=True)], reads=[r_cv32, r_ident], writes=[r_psA])
                    S.op("vector", lambda e: e.tensor_copy(cvT[:].rearrange("p k g -> p g k"),
                                                           psA[:, 0:64].rearrange("p (g k) -> p g k", g=2)),
                         reads=[r_psA], writes=[r_cvT])
                    S.op("scalar", lambda e: e.activation(out=scb[:], in_=cvT[:], func=AF.Silu),
                         reads=[r_cvT], writes=[r_scb])
                    load_vec_T(ph, b_mod[l, :], NMOD * KC, bmT, r_bmT, psA, r_psA)
                    for i, nv in enumerate((norm_ffn1, norm_mix, norm_ffn2)):
                        load_vec_T(ph, nv[l, :], KC, nrmT[:, i, :], r_nrmT, psA, r_psA)

            def mod_gen(ph, l, j_lo, j_hi, parts):
                wps = [ph.sb("mwp%d" % i, [128, KC, 128], BF16) for i in range(2)]
                psB = [ph.ps("mpsB%d" % i, [128, 512]) for i in range(2)]
                wv = w_mod[l].rearrange("(kc p) n -> p kc n", p=128)
                for n_, j in enumerate(range(j_lo, j_hi)):
                    wp, r_wp = wps[n_ % 2]
                    pst, r_ps = psB[n_ % 2]
                    for s2 in range(2):
                        S.dma("gpsimd", wp[:, s2 * 16:(s2 + 1) * 16, :], wv[:, s2 * 16:(s2 + 1) * 16, j * 128:(j + 1) * 128],
                              writes=[r_wp])
                    S.pe([lambda e, pst=pst, wp=wp, kc=kc: e.matmul(pst[:, 0:2], wp[:, kc, :], scb[:, kc, :],
                                                                    start=(kc == 0), stop=(kc == KC - 1)) for kc in range(KC)],
                         reads=[r_wp, r_scb], writes=[r_ps])
                    S.op("vector", lambda e, pst=pst, j=j: e.tensor_tensor(
                        modT[:, j, :], pst[:, 0:2], bmT[:, j:j + 1].broadcast_to([128, 2]), ALU.add),
                        reads=[r_ps, r_bmT], writes=[r_modT])
                    yield
                for i in parts:
                    sc = modT[:, (3 * i + 1) * KC:(3 * i + 2) * KC, :]
                    gg = modT[:, (3 * i + 2) * KC:(3 * i + 3) * KC, :]
                    S.op("vector", lambda e, i=i, sc=sc: e.scalar_tensor_tensor(
                        coefA[:, i, :, :], sc, 1.0,
                        nrmT[:, i, :].rearrange("p (k o) -> p k o", o=1).broadcast_to([128, KC, 2]),
                        ALU.add, ALU.mult), reads=[r_modT, r_nrmT], writes=[r_coefA])
                    S.op("vector", lambda e, i=i, gg=gg: e.tensor_scalar(
                        gT[:, i, :, :], gg, 0.5 if i != 1 else 1.0, None, ALU.mult),
                        reads=[r_modT], writes=[r_gT])

            def phase_norm(ph, AT, r_AT, i_norm):
                with Phase(S) as p2:
                    xts = [p2.sb("nxt%d" % i, [128, KC, 128]) for i in range(2)]
                    sqs = [p2.sb("nsq%d" % i, [128, KC, 128], BF16) for i in range(2)]
                    rss = [p2.sb("nrs%d" % i, [128, 128]) for i in range(2)]
                    pss = [p2.ps("nps%d" % i, [128, 512]) for i in range(2)]
                    for tt in range(NTILE):
                        g = tt // 8
                        xt, r_xt = xts[tt % 2]; sq, r_sq = sqs[tt % 2]; t1, r_t1 = xt, r_xt
                        rs, r_rs = rss[tt % 2]; pst, r_ps = pss[tt % 2]
                        S.dma("sync", xt[:], XT3[:, :, tt * 128:(tt + 1) * 128], writes=[r_xt])
                        S.op("scalar", lambda e, sq=sq, xt=xt: e.activation(out=sq[:], in_=xt[:], func=AF.Square),
                             reads=[r_xt], writes=[r_sq])
                        S.pe([lambda e, pst=pst, sq=sq, kc=kc: e.matmul(pst[:, 0:128], onesb[:], sq[:, kc, :],
                                                                        start=(kc == 0), stop=(kc == KC - 1))
                              for kc in range(KC)], reads=[r_sq, r_onesb], writes=[r_ps])
                        S.op("scalar", lambda e, rs=rs, pst=pst: e.activation(out=rs[:], in_=pst[:, 0:128], func=AF.Sqrt,
                                                                              bias=epsc[:, 0:1], scale=1.0 / D),
                             reads=[r_ps, r_epsc], writes=[r_rs])
                        S.op("vector", lambda e, rs=rs: e.reciprocal(rs[:], rs[:]), reads=[r_rs], writes=[r_rs])
                        S.op("vector", lambda e, t1=t1, xt=xt, rs=rs: e.tensor_tensor(
                            t1[:], xt[:], rs[:].rearrange("p (o t) -> p o t", o=1).broadcast_to([128, KC, 128]), ALU.mult),
                            reads=[r_xt, r_rs], writes=[r_t1])
                        S.op("gpsimd", lambda e, t1=t1, g=g: e.tensor_tensor(
                            t1[:], t1[:], coefA[:, i_norm, :, g:g + 1].broadcast_to([128, KC, 128]), ALU.mult),
                            reads=[r_t1, r_coefA], writes=[r_t1])
                        sh = modT[:, (3 * i_norm) * KC:(3 * i_norm + 1) * KC, g:g + 1]
                        S.op("vector", lambda e, t1=t1, tt=tt, sh=sh: e.tensor_tensor(
                            AT[:, :, tt * 128:(tt + 1) * 128], t1[:], sh.broadcast_to([128, KC, 128]), ALU.add),
                            reads=[r_t1, r_modT], writes=[r_AT[tt]])

            def phase_ffn(l, which, hosted=None):
                i_norm = 0 if which == 0 else 2
                wi = ffn_w_in[which][l].rearrange("(kc p) n -> p kc n", p=128)
                wo = ffn_w_out[which][l]
                HT3 = HT.rearrange("(fc p) t -> p fc t", p=128)
                with Phase(S) as ph:
                    AT, r_AT = ph.sb("AT", [128, KC, NT], BF16, nres=NTILE)
                    phase_norm(ph, AT, r_AT, i_norm)
                    wps = [ph.sb("fwp%d" % i, [128, KC, 256], BF16) for i in range(2)]
                    gen = mod_gen(ph, *hosted) if hosted is not None else None
                    per_blk = 0 if hosted is None else -(-(hosted[2] - hosted[1]) // 64)
                    psG = [ph.ps("psG%d" % i, [128, 512]) for i in range(2)]
                    psU = [ph.ps("psU%d" % i, [128, 512]) for i in range(2)]
                    sgs = [ph.sb("sg%d" % i, [128, 512]) for i in range(2)]
                    hbs = [ph.sb("hb%d" % i, [128, NT], BF16) for i in range(2)]
                    n = 0
                    for j in range(FC):
                        wp, r_wp = wps[j % 2]
                        for half, c0 in ((0, j * 128), (1, DFF + j * 128)):
                            for s2 in range(2):
                                S.dma("gpsimd", wp[:, s2 * 16:(s2 + 1) * 16, half * 128:(half + 1) * 128],
                                      wi[:, s2 * 16:(s2 + 1) * 16, c0:c0 + 128], writes=[r_wp])
                        hb, r_hb = hbs[j % 2]
                        for tb in range(4):
                            pg, r_pg = psG[n % 2]; pu, r_pu = psU[n % 2]; sg, r_sg = sgs[n % 2]
                            n += 1
                            rd = [r_wp] + r_AT[tb * 4:(tb + 1) * 4]
                            S.pe([lambda e, pg=pg, wp=wp, kc=kc, tb=tb: e.matmul(
                                pg[:], wp[:, kc, 0:128], AT[:, kc, tb * 512:(tb + 1) * 512],
                                start=(kc == 0), stop=(kc == KC - 1)) for kc in range(KC)], reads=rd, writes=[r_pg])
                            S.pe([lambda e, pu=pu, wp=wp, kc=kc, tb=tb: e.matmul(
                                pu[:], wp[:, kc, 128:256], AT[:, kc, tb * 512:(tb + 1) * 512],
                                start=(kc == 0), stop=(kc == KC - 1)) for kc in range(KC)], reads=rd, writes=[r_pu])
                            S.op("scalar", lambda e, sg=sg, pg=pg: e.activation(out=sg[:], in_=pg[:], func=AF.Silu),
                                 reads=[r_pg], writes=[r_sg])
                            S.op("vector", lambda e, hb=hb, sg=sg, pu=pu, tb=tb: e.tensor_tensor(
                                hb[:, tb * 512:(tb + 1) * 512], sg[:], pu[:], ALU.mult),
                                reads=[r_sg, r_pu], writes=[r_hb])
                        S.dma("sync", HT3[:, j, :], hb[:], reads=[r_hb])
                        if gen is not None:
                            for _ in range(per_blk):
                                if next(gen, "done") == "done":
                                    gen = None
                                    break
                    if gen is not None:
                        for _ in gen:
                            pass
                with Phase(S) as ph:
                    KP = 8
                    panels = [(f0, min(KP, FC - f0)) for f0 in range(0, FC, KP)]
                    yacc, r_y = ph.sb("yacc", [128, 8, NT], F32, nres=32)
                    hps = [ph.sb("hp%d" % i, [128, KP, NT], BF16) for i in range(2)]
                    wops = [ph.sb("wop%d" % i, [128, KP, 1024], BF16) for i in range(2)]
                    pss = [ph.ps("yps%d" % i, [128, 512]) for i in range(4)]
                    n = 0
                    for dq in range(4):
                        for db in range(8):
                            ch = dq * 8 + db
                            for tb in range(4):
                                S.dma("sync", yacc[:, db, tb * 512:(tb + 1) * 512], XT3[:, ch, tb * 512:(tb + 1) * 512],
                                      writes=[r_y[db * 4 + tb]])
                        for pi, (f0, kp) in enumerate(panels):
                            hp, r_hp = hps[(dq * len(panels) + pi) % 2]
                            wop, r_wop = wops[(dq * len(panels) + pi) % 2]
                            S.dma("gpsimd", hp[:, 0:kp, :], HT3[:, f0:f0 + kp, :], writes=[r_hp])
                            S.dma("gpsimd", wop[:, 0:kp, :],
                                  wo[f0 * 128:(f0 + kp) * 128, dq * 1024:(dq + 1) * 1024].rearrange("(fc p) n -> p fc n", p=128),
                                  writes=[r_wop])
                            for db in range(8):
                                ch = dq * 8 + db
                                for tb in range(4):
                                    g = tb // 2
                                    pst, r_ps = pss[n % 4]
                                    n += 1
                                    S.pe([lambda e, pst=pst, wop=wop, hp=hp, k=k, kp=kp, db=db, tb=tb: e.matmul(
                                        pst[:], wop[:, k, db * 128:(db + 1) * 128], hp[:, k, tb * 512:(tb + 1) * 512],
                                        start=(k == 0), stop=(k == kp - 1)) for k in range(kp)],
                                        reads=[r_wop, r_hp], writes=[r_ps])
                                    ry = r_y[db * 4 + tb]
                                    ya = yacc[:, db, tb * 512:(tb + 1) * 512]
                                    S.op("vector", lambda e, ya=ya, pst=pst, ch=ch, g=g: e.scalar_tensor_tensor(
                                        ya, pst[:], gT[:, i_norm, ch, g:g + 1], ya, ALU.mult, ALU.add),
                                        reads=[r_ps, ry, r_gT], writes=[ry])
                        for db in range(8):
                            ch = dq * 8 + db
                            for tb in range(4):
                                S.dma("scalar", XT3[:, ch, tb * 512:(tb + 1) * 512], yacc[:, db, tb * 512:(tb + 1) * 512],
                                      reads=[r_y[db * 4 + tb]])

            def rope_ops(eng, xv, ov, ctab, stab, nvec, qd, tmpv, rd, wr, r_tmp):
                cb = ctab.rearrange("p (o a) q -> p o a q", o=1).broadcast_to([128, nvec, 2, qd])
                sb_ = stab.rearrange("p (o a) q -> p o a q", o=1).broadcast_to([128, nvec, 2, qd])
                x0 = xv[:, :, :, 0, :]; x1 = xv[:, :, :, 1, :]
                t0 = tmpv[:, 0]; t1 = tmpv[:, 1]
                S.op(eng, lambda e: e.tensor_tensor(t0, x1, sb_, ALU.mult), reads=rd, writes=[r_tmp])
                S.op(eng, lambda e: e.tensor_tensor(t1, x0, sb_, ALU.mult), reads=rd, writes=[r_tmp])
                S.op(eng, lambda e: e.tensor_tensor(ov[:, :, :, 0, :], x0, cb, ALU.mult), reads=rd, writes=wr)
                S.op(eng, lambda e: e.tensor_tensor(ov[:, :, :, 1, :], x1, cb, ALU.mult), reads=rd, writes=wr)
                S.op(eng, lambda e: e.tensor_tensor(ov[:, :, :, 0, :], ov[:, :, :, 0, :], t0, ALU.subtract),
                     reads=[r_tmp] + wr, writes=wr)
                S.op(eng, lambda e: e.tensor_tensor(ov[:, :, :, 1, :], ov[:, :, :, 1, :], t1, ALU.add),
                     reads=[r_tmp] + wr, writes=wr)

            def phase_mixer(l):
                win = w_in[l].rearrange("(kc p) n -> p kc n", p=128)
                RAW3 = RAW.rearrange("(tt p) c -> p tt c", p=128)
                with Phase(S) as ph:
                    AT, r_AT = ph.sb("AT", [128, KC, NT], BF16, nres=NTILE)
                    phase_norm(ph, AT, r_AT, 1)
                    wps = [ph.sb("iwp%d" % i, [128, KC, 256], BF16) for i in range(2)]
                    sts = [ph.sb("ist%d" % i, [128, NTILE, 256]) for i in range(2)]
                    pss = [ph.ps("ips%d" % i, [128, 512]) for i in range(4)]
                    npan = (INC + 255) // 256
                    n = 0
                    for pn in range(npan):
                        c0 = pn * 256
                        w = min(256, INC - c0)
                        wp, r_wp = wps[pn % 2]
                        st, r_st = sts[pn % 2]
                        for s4 in range(4):
                            S.dma("gpsimd", wp[:, s4 * 8:(s4 + 1) * 8, 0:w], win[:, s4 * 8:(s4 + 1) * 8, c0:c0 + w], writes=[r_wp])
                        for tt in range(NTILE):
                            pst, r_ps = pss[n % 4]
                            n += 1
                            S.pe([lambda e, pst=pst, wp=wp, kc=kc, tt=tt, w=w: e.matmul(
                                pst[:, 0:w], AT[:, kc, tt * 128:(tt + 1) * 128], wp[:, kc, 0:w],
                                start=(kc == 0), stop=(kc == KC - 1)) for kc in range(KC)],
                                reads=[r_wp, r_AT[tt]], writes=[r_ps])
                            if tt % 2 == 0:
                                S.op("vector", lambda e, st=st, pst=pst, tt=tt, w=w: e.tensor_copy(st[:, tt, 0:w], pst[:, 0:w]),
                                     reads=[r_ps], writes=[r_st])
                            else:
                                S.op("scalar", lambda e, st=st, pst=pst, tt=tt, w=w: e.activation(
                                    out=st[:, tt, 0:w], in_=pst[:, 0:w], func=AF.Copy), reads=[r_ps], writes=[r_st])
                        S.dma("sync", RAW3[:, :, c0:c0 + w], st[:, :, 0:w], reads=[r_st])

                with Phase(S) as ph:
                    wuq, r_wuq = ph.sb("wuq", [128, 8, 2304], BF16)
                    wukv, r_wukv = ph.sb("wukv", [128, 4, 3072], BF16)
                    S.dma("gpsimd", wuq[:], w_uq[l].rearrange("(kc p) n -> p kc n", p=128), writes=[r_wuq])
                    S.dma("gpsimd", wukv[:], w_ukv[l].rearrange("(kc p) n -> p kc n", p=128), writes=[r_wukv])
                    gb, r_gb = ph.sb("gb", [128, 2, 128])
                    mqb, r_mqb = ph.sb("mqb", [128, 1024])
                    mkb, r_mkb = ph.sb("mkb", [128, 512])
                    S.dma("sync", gb[:, 0, :], gqn[l, :].partition_broadcast(128), writes=[r_gb])
                    S.dma("sync", gb[:, 1, :], gkn[l, :].partition_broadcast(128), writes=[r_gb])
                    S.dma("sync", mqb[:], mqn[l, :].partition_broadcast(128), writes=[r_mqb])
                    S.dma("sync", mkb[:], mkvn[l, :].partition_broadcast(128), writes=[r_mkb])
                    raws = [ph.sb("raw%d" % i, [128, INC]) for i in range(1)]
                    wk, r_wk = ph.sb("wk", [128, 3584])
                    sq, r_sq = ph.sb("sq", [128, 3584])
                    ss, r_ss = ph.sb("ss", [128, 20])
                    rtmp, r_rtmp = ph.sb("rtmp", [128, 2, 1536])
                    cs64, r_cs64 = ph.sb("cs64", [128, 64])
                    cs128, r_cs128 = ph.sb("cs128", [128, 128])
                    Qp, r_Qp = ph.sb("Qp", [128, QW], BF16)
                    Kp, r_Kp = ph.sb("Kp", [128, KW], BF16)
                    Vp, r_Vp = ph.sb("Vp", [128, VW], BF16)
                    cnb, r_cnb = ph.sb("cnb", [128, 1536], BF16)
                    cnT, r_cnT = ph.sb("cnT", [128, 12, 128], BF16)
                    mqf, r_mqf = ph.sb("mqf", [128, 2304])
                    QTs, r_QTs = ph.sb("QTs", [128, 38, 128], BF16)
                    KTs, r_KTs = ph.sb("KTs", [128, 25, 128], BF16)
                    pT = [ph.ps("pT%d" % i, [128, 1024], BF16) for i in range(2)]
                    pM = [ph.ps("pM%d" % i, [128, 512]) for i in range(3)]
                    QT3 = QT.rearrange("(c p) t -> p c t", p=128)
                    KTm = KT[0:3072, :].rearrange("(c p) t -> p c t", p=128)
                    nT = [0]
                    nM = [0]

                    def transposes(src, r_src, nblk, dst, r_dst, last_w=128):
                        b = 0
                        while b < nblk:
                            nb = min(8, nblk - b)
                            pt, r_pt = pT[nT[0] % 2]
                            nT[0] += 1
                            fns = []
                            for i in range(nb):
                                wd = last_w if (b + i == nblk - 1) else 128
                                fns.append(lambda e, pt=pt, i=i, c=b + i, wd=wd: e.transpose(
                                    pt[0:wd, i * 128:(i + 1) * 128], src[:, c * 128:c * 128 + wd], identb[:]))
                            S.pe(fns, reads=[r_src, r_identb], writes=[r_pt])
                            S.op("scalar" if (nT[0] % 2) else "vector",
                                 (lambda e, pt=pt, b=b, nb=nb: e.activation(
                                     out=dst[:, b:b + nb, :], in_=pt[:, 0:nb * 128].rearrange("p (a t) -> p a t", a=nb), func=AF.Copy))
                                 if (nT[0] % 2) else
                                 (lambda e, pt=pt, b=b, nb=nb: e.tensor_copy(
                                     dst[:, b:b + nb, :], pt[:, 0:nb * 128].rearrange("p (a t) -> p a t", a=nb))),
                                 reads=[r_pt], writes=[r_dst])
                            b += nb

                    def mla_kv(krow0):
                        for nb in range(6):
                            pm, r_pm = pM[nM[0] % 3]
                            nM[0] += 1
                            S.pe([lambda e, pm=pm, kc=kc, nb=nb: e.matmul(
                                pm[:], cnT[:, 8 + kc, :], wukv[:, kc, nb * 512:(nb + 1) * 512],
                                start=(kc == 0), stop=(kc == 3)) for kc in range(4)],
                                reads=[r_cnT, r_wukv], writes=[r_pm])
                            pv = pm[:].rearrange("p (h c) -> p h c", h=2)
                            kdst = Kp[:, 1536 + nb * 256:1536 + (nb + 1) * 256].rearrange("p (h c) -> p h c", h=2)
                            vdst = Vp[:, 1536 + nb * 256:1536 + (nb + 1) * 256].rearrange("p (h c) -> p h c", h=2)
                            if nb % 2 == 0:
                                S.op("vector", lambda e, pv=pv, kdst=kdst: e.tensor_copy(kdst, pv[:, :, 0:128]), reads=[r_pm], writes=[r_Kp])
                                S.op("vector", lambda e, pv=pv, vdst=vdst: e.tensor_copy(vdst, pv[:, :, 128:256]), reads=[r_pm], writes=[r_Vp])
                            else:
                                S.op("scalar", lambda e, pv=pv, kdst=kdst: e.activation(out=kdst, in_=pv[:, :, 0:128], func=AF.Copy),
                                     reads=[r_pm], writes=[r_Kp])
                                S.op("scalar", lambda e, pv=pv, vdst=vdst: e.activation(out=vdst, in_=pv[:, :, 128:256], func=AF.Copy),
                                     reads=[r_pm], writes=[r_Vp])

                    def store_kv(krow0):
                        transposes(Kp, r_Kp, 25, KTs, r_KTs, last_w=64)
                        for c0, c1 in ((0, 8), (8, 16), (16, 24)):
                            S.dma("sync", KTm[:, c0:c1, krow0:krow0 + 128], KTs[:, c0:c1, :], reads=[r_KTs])
                        S.dma("sync", KT[3072:3136, krow0:krow0 + 128], KTs[0:64, 24, :], reads=[r_KTs])
                        S.dma("sync", VV[krow0:krow0 + 128, :], Vp[:], reads=[r_Vp])

                    for ct in range(2):
                        raw, r_raw = raws[0]
                        rows = slice(ct * 128, (ct + 1) * 128)
                        S.dma("sync", raw[:, 0:1024], cdk[l, rows, :], writes=[r_raw])
                        S.dma("sync", raw[:, 1024:1536], cgk[l, rows, :], writes=[r_raw])
                        S.dma("sync", raw[:, 1536:1600], ckpe[l, rows, :], writes=[r_raw])
                        S.dma("sync", raw[:, 2048:3072], cdv[l, rows, :], writes=[r_raw])
                        S.dma("sync", raw[:, 3072:3584], cgv[l, rows, :], writes=[r_raw])
                        S.dma("sync", raw[:, 4096:4608], cckv[l, rows, :], writes=[r_raw])
                        S.op("vector", lambda e, raw=raw: e.tensor_copy(Kp[:, 0:1536], raw[:, 0:1536]), reads=[r_raw], writes=[r_Kp])
                        S.op("vector", lambda e, raw=raw: e.tensor_copy(Kp[:, 3072:3136], raw[:, 1536:1600]), reads=[r_raw], writes=[r_Kp])
                        S.op("scalar", lambda e, raw=raw: e.activation(out=Vp[:, 0:1536], in_=raw[:, 2048:3584], func=AF.Copy),
                             reads=[r_raw], writes=[r_Vp])
                        S.op("vector", lambda e, raw=raw: e.tensor_copy(cnb[:, 1024:1536], raw[:, 4096:4608]), reads=[r_raw], writes=[r_cnb])
                        transposes(cnb[:, 1024:1536], r_cnb, 4, cnT[:, 8:12, :], r_cnT)
                        mla_kv(0)
                        store_kv(1024 + ct * 128)

                    for tt in range(NTILE):
                        g, lt = tt // 8, tt % 8
                        raw, r_raw = raws[0]
                        S.dma("sync", raw[:], RAW[tt * 128:(tt + 1) * 128, :], writes=[r_raw])
                        if g == 1:
                            S.dma("sync", cs64[:], cs64_d[lt * 128:(lt + 1) * 128, :], writes=[r_cs64])
                            S.dma("sync", cs128[:], cs128_d[lt * 128:(lt + 1) * 128, :], writes=[r_cs128])
                        S.op("gpsimd", lambda e, raw=raw: e.tensor_tensor(sq[:, 0:2048], raw[:, 3072:5120], raw[:, 3072:5120], ALU.mult),
                             reads=[r_raw], writes=[r_sq])
                        S.op("gpsimd", lambda e, raw=raw: e.tensor_tensor(sq[:, 2048:3584], raw[:, 5632:7168], raw[:, 5632:7168], ALU.mult),
                             reads=[r_raw], writes=[r_sq])
                        S.op("vector", lambda e: e.tensor_reduce(ss[:, 0:16], sq[:, 0:2048].rearrange("p (h c) -> p h c", h=16), AX.X, ALU.add),
                             reads=[r_sq], writes=[r_ss])
                        S.op("vector", lambda e: e.tensor_reduce(ss[:, 16:17], sq[:, 2048:3072], AX.X, ALU.add), reads=[r_sq], writes=[r_ss])
                        S.op("vector", lambda e: e.tensor_reduce(ss[:, 17:18], sq[:, 3072:3584], AX.X, ALU.add), reads=[r_sq], writes=[r_ss])
                        S.op("scalar", lambda e: e.activation(out=ss[:, 0:16], in_=ss[:, 0:16], func=AF.Sqrt, bias=epsc[:, 0:1], scale=1.0 / 128),
                             reads=[r_ss, r_epsc], writes=[r_ss])
                        S.op("scalar", lambda e: e.activation(out=ss[:, 16:17], in_=ss[:, 16:17], func=AF.Sqrt, bias=epsc[:, 0:1], scale=1.0 / 1024),
                             reads=[r_ss, r_epsc], writes=[r_ss])
                        S.op("scalar", lambda e: e.activation(out=ss[:, 17:18], in_=ss[:, 17:18], func=AF.Sqrt, bias=epsc[:, 0:1], scale=1.0 / 512),
                             reads=[r_ss, r_epsc], writes=[r_ss])
                        S.op("vector", lambda e: e.reciprocal(ss[:, 0:18], ss[:, 0:18]), reads=[r_ss], writes=[r_ss])
                        S.op("vector", lambda e, raw=raw: e.tensor_tensor(
                            wk[:, 0:2048].rearrange("p (h c) -> p h c", h=16), raw[:, 3072:5120].rearrange("p (h c) -> p h c", h=16),
                            ss[:, 0:16].rearrange("p (h o) -> p h o", o=1).broadcast_to([128, 16, 128]), ALU.mult),
                            reads=[r_raw, r_ss], writes=[r_wk])
                        S.op("gpsimd", lambda e: e.tensor_tensor(
                            wk[:, 0:1536].rearrange("p (h c) -> p h c", h=12), wk[:, 0:1536].rearrange("p (h c) -> p h c", h=12),
                            gb[:, 0:1, :].broadcast_to([128, 12, 128]), ALU.mult), reads=[r_wk, r_gb], writes=[r_wk])
                        S.op("gpsimd", lambda e: e.tensor_tensor(
                            wk[:, 1536:2048].rearrange("p (h c) -> p h c", h=4), wk[:, 1536:2048].rearrange("p (h c) -> p h c", h=4),
                            gb[:, 1:2, :].broadcast_to([128, 4, 128]), ALU.mult), reads=[r_wk, r_gb], writes=[r_wk])
                        S.op("vector", lambda e, raw=raw: e.scalar_tensor_tensor(
                            wk[:, 2048:3072], raw[:, 5632:6656], ss[:, 16:17], mqb[:], ALU.mult, ALU.mult),
                            reads=[r_raw, r_ss, r_mqb], writes=[r_wk])
                        S.op("vector", lambda e, raw=raw: e.scalar_tensor_tensor(
                            wk[:, 3072:3584], raw[:, 6656:7168], ss[:, 17:18], mkb[:], ALU.mult, ALU.mult),
                            reads=[r_raw, r_ss, r_mkb], writes=[r_wk])
                        S.op("scalar", lambda e: e.activation(out=cnb[:], in_=wk[:, 2048:3584], func=AF.Copy), reads=[r_wk], writes=[r_cnb])
                        if g == 0:
                            b, t0 = lt // 2, (lt % 2) * 128
                            S.dma("sync", ndk[b, l, t0:t0 + 128, :], raw[:, 1024:2048], reads=[r_raw])
                            S.dma("sync", ndv[b, l, t0:t0 + 128, :], raw[:, 2048:3072], reads=[r_raw])
                            S.dma("sync", ngv[b, l, t0:t0 + 128, :], raw[:, 5120:5632], reads=[r_raw])
                            S.dma("sync", nkpe[b, l, t0:t0 + 128, :], raw[:, 7168:7232], reads=[r_raw])
                            S.dma("sync", ngk[b, l, t0:t0 + 128, :], wk[:, 1536:2048], reads=[r_wk])
                            S.dma("sync", nckv[b, l, t0:t0 + 128, :], wk[:, 3072:3584], reads=[r_wk])
                            S.op("vector", lambda e, raw=raw: e.tensor_copy(Qp[:, 0:1024], raw[:, 0:1024]), reads=[r_raw], writes=[r_Qp])
                            S.op("scalar", lambda e: e.activation(out=Qp[:, 1024:2560], in_=wk[:, 0:1536], func=AF.Copy), reads=[r_wk], writes=[r_Qp])
                            S.op("vector", lambda e, raw=raw: e.tensor_copy(Kp[:, 0:1024], raw[:, 1024:2048]), reads=[r_raw], writes=[r_Kp])
                            S.op("scalar", lambda e: e.activation(out=Kp[:, 1024:1536], in_=wk[:, 1536:2048], func=AF.Copy), reads=[r_wk], writes=[r_Kp])
                            S.op("vector", lambda e, raw=raw: e.tensor_copy(Kp[:, 3072:3136], raw[:, 7168:7232]), reads=[r_raw], writes=[r_Kp])
                        else:
                            c64 = cs64[:, 0:32].rearrange("p (a q) -> p a q", a=2); s64 = cs64[:, 32:64].rearrange("p (a q) -> p a q", a=2)
                            c128 = cs128[:, 0:64].rearrange("p (a q) -> p a q", a=2); s128 = cs128[:, 64:128].rearrange("p (a q) -> p a q", a=2)

                            def v5(ap, nvec, qd):
                                return ap.rearrange("p (v a b q) -> p v a b q", v=nvec, a=2, b=2, q=qd)

                            def tv(n, qd):
                                return rtmp[:, :, 0:n * 2 * qd].rearrange("p t (v a q) -> p t v a q", v=n, a=2, q=qd)
                            rope_ops("vector", v5(raw[:, 0:1024], 16, 16), v5(Qp[:, 0:1024], 16, 16), c64, s64, 16, 16,
                                     tv(16, 16), [r_raw, r_cs64], [r_Qp], r_rtmp)
                            rope_ops("gpsimd", v5(raw[:, 1024:2048], 16, 16), v5(Kp[:, 0:1024], 16, 16), c64, s64, 16, 16,
                                     tv(16, 16), [r_raw, r_cs64], [r_Kp], r_rtmp)
                            rope_ops("vector", v5(wk[:, 0:1536], 12, 32), v5(Qp[:, 1024:2560], 12, 32), c128, s128, 12, 32,
                                     tv(12, 32), [r_wk, r_cs128], [r_Qp], r_rtmp)
                            rope_ops("gpsimd", v5(wk[:, 1536:2048], 4, 32), v5(Kp[:, 1024:1536], 4, 32), c128, s128, 4, 32,
                                     tv(4, 32), [r_wk, r_cs128], [r_Kp], r_rtmp)
                            rope_ops("vector", v5(raw[:, 7168:7232], 1, 16), v5(Kp[:, 3072:3136], 1, 16), c64, s64, 1, 16,
                                     tv(1, 16), [r_raw, r_cs64], [r_Kp], r_rtmp)
                        S.op("scalar", lambda e, raw=raw: e.activation(out=Vp[:, 0:1024], in_=raw[:, 2048:3072], func=AF.Copy), reads=[r_raw], writes=[r_Vp])
                        S.op("scalar", lambda e, raw=raw: e.activation(out=Vp[:, 1024:1536], in_=raw[:, 5120:5632], func=AF.Copy), reads=[r_raw], writes=[r_Vp])
                        transposes(cnb, r_cnb, 12, cnT, r_cnT)
                        for nb in range(5):
                            w = 512 if nb < 4 else 256
                            pm, r_pm = pM[nM[0] % 3]
                            nM[0] += 1
                            S.pe([lambda e, pm=pm, kc=kc, nb=nb, w=w: e.matmul(
                                pm[:, 0:w], cnT[:, kc, :], wuq[:, kc, nb * 512:nb * 512 + w],
                                start=(kc == 0), stop=(kc == 7)) for kc in range(8)], reads=[r_cnT, r_wuq], writes=[r_pm])
                            S.op("vector" if nb % 2 else "scalar",
                                 (lambda e, pm=pm, nb=nb, w=w: e.tensor_copy(mqf[:, nb * 512:nb * 512 + w], pm[:, 0:w])) if nb % 2 else
                                 (lambda e, pm=pm, nb=nb, w=w: e.activation(out=mqf[:, nb * 512:nb * 512 + w], in_=pm[:, 0:w], func=AF.Copy)),
                                 reads=[r_pm], writes=[r_mqf])
                        mq3 = mqf[:].rearrange("p (h c) -> p h c", h=12)
                        Qm3 = Qp[:, 2560:4864].rearrange("p (h c) -> p h c", h=12)
                        S.op("scalar", lambda e: e.activation(out=Qm3[:, :, 0:128], in_=mq3[:, :, 0:128], func=AF.Copy), reads=[r_mqf], writes=[r_Qp])
                        if g == 0:
                            S.op("vector", lambda e: e.tensor_copy(Qm3[:, :, 128:192], mq3[:, :, 128:192]), reads=[r_mqf], writes=[r_Qp])
                        else:
                            def v5s(ap3):
                                return ap3.rearrange("p v (a b q) -> p v a b q", a=2, b=2, q=16)
                            rope_ops("vector", v5s(mq3[:, :, 128:192]), v5s(Qm3[:, :, 128:192]), c64, s64, 12, 16,
                                     tv(12, 16), [r_mqf, r_cs64], [r_Qp], r_rtmp)
                        mla_kv(0)
                        transposes(Qp, r_Qp, 38, QTs, r_QTs)
                        for c0, c1 in ((0, 10), (10, 20), (20, 30), (30, 38)):
                            S.dma("sync", QT3[:, c0:c1, tt * 128:(tt + 1) * 128], QTs[:, c0:c1, :], reads=[r_QTs])
                        store_kv(tt * 128 if g == 0 else 1280 + lt * 128)

                with Phase(S) as ph:
                    OT, r_OT = ph.sb("OT", [128, KC, NT], BF16, nres=KC)
                    with Phase(S) as pa:
                        lamt, r_lam = pa.sb("lamt", [128, 8])
                        lv, r_lv = pa.sb("lv", [128, 4, 64])
                        sgn, r_sgn = pa.sb("sgn", [128, 1])
                        for i, v in enumerate((lq1, lk1, lq2, lk2)):
                            S.dma("sync", lv[:, i, :], v[l, :].partition_broadcast(128), writes=[r_lv])
                        S.op("vector", lambda e: e.tensor_tensor(lv[:, 0, :], lv[:, 0, :], lv[:, 1, :], ALU.mult), reads=[r_lv], writes=[r_lv])
                        S.op("vector", lambda e: e.tensor_tensor(lv[:, 2, :], lv[:, 2, :], lv[:, 3, :], ALU.mult), reads=[r_lv], writes=[r_lv])
                        S.op("vector", lambda e: e.tensor_reduce(lamt[:, 0:1], lv[:, 0, :], AX.X, ALU.add), reads=[r_lv], writes=[r_lam])
                        S.op("vector", lambda e: e.tensor_reduce(lamt[:, 1:2], lv[:, 2, :], AX.X, ALU.add), reads=[r_lv], writes=[r_lam])
                        S.op("scalar", lambda e: e.activation(out=lamt[:, 2:4], in_=lamt[:, 0:2], func=AF.Exp), reads=[r_lam], writes=[r_lam])
                        S.op("vector", lambda e: e.scalar_tensor_tensor(lamt[:, 4:5], lamt[:, 3:4], -lam_init(l), lamt[:, 2:3],
                                                                        ALU.add, ALU.subtract), reads=[r_lam], writes=[r_lam])
                        S.dma("sync", sgn[:], subln[l, :].rearrange("(p o) -> p o", o=1), writes=[r_sgn])
                        S.op("vector", lambda e: e.tensor_scalar(sgn[:], sgn[:], 1.0 - lam_init(l), None, ALU.mult), reads=[r_sgn], writes=[r_sgn])
                        neglam = lamt[:, 4:5]

                        NB = 3
                        kts = [pa.sb("kt%d" % i, [128, 1280], BF16) for i in range(NB)]
                        kpes, r_kpes = pa.sb("kpes", [64, 1280], BF16)
                        vts = [pa.sb("vt%d" % i, [128, 10, 128], BF16) for i in range(NB)]
                        qts = [pa.sb("qt%d" % i, [128, 1024], BF16) for i in range(NB)]
                        qpes = [pa.sb("qpe%d" % i, [64, 1024], BF16) for i in range(NB)]
                        ets = [pa.sb("et%d" % i, [128, 512], BF16) for i in range(3)]
                        rcs = [pa.sb("rc%d" % i, [128, 512]) for i in range(2)]
                        o0s, r_o0 = pa.sb("o0s", [128, 512])
                        o1s, r_o1 = pa.sb("o1s", [128, 512])
                        sqb, r_sqb = pa.sb("sqb", [128, 512], BF16)
                        psS = [pa.ps("psS%d" % i, [128, 512]) for i in range(3)]
                        psO = [pa.ps("psO%d" % i, [128, 512]) for i in range(2)]
                        psR = [pa.ps("psR%d" % i, [128, 512]) for i in range(2)]
                        psN, r_psN = pa.ps("psN", [128, 512])
                        cnt = {"s": 0, "o": 0, "e": 0, "k": 0, "v": 0, "q": 0}

                        def attend(qparts, kparts, vt, r_v, nkt, T, scale, finish, rd_extra):
                            for q0 in range(0, T, 512):
                                w = min(512, T - q0)
                                po, r_po = psO[cnt["o"] % 2]
                                pr, r_pr = psR[cnt["o"] % 2]
                                rc, r_rc = rcs[cnt["o"] % 2]
                                cnt["o"] += 1
                                np_ = len(qparts)
                                stage = {}

                                def score(kt):
                                    pS, r_pS = psS[cnt["s"] % 3]
                                    et, r_et = ets[cnt["s"] % 3]
                                    cnt["s"] += 1
                                    S.pe([lambda e, pS=pS, i=i, kt=kt, w=w, q0=q0: e.matmul(
                                        pS[:, 0:w], kparts[i][0][0:kparts[i][1], kt * 128:(kt + 1) * 128],
                                        qparts[i][0][0:qparts[i][1], q0:q0 + w], start=(i == 0), stop=(i == np_ - 1))
                                        for i in range(np_)], reads=rd_extra, writes=[r_pS])
                                    S.op("scalar", lambda e, et=et, pS=pS, w=w: e.activation(
                                        out=et[:, 0:w], in_=pS[:, 0:w], func=AF.Exp, scale=scale), reads=[r_pS], writes=[r_et])
                                    stage[kt] = (et, r_et)

                                def pv(kt):
                                    et, r_et = stage.pop(kt)
                                    S.pe([lambda e, et=et, kt=kt, po=po, w=w: e.matmul(
                                        po[:, 0:w], vt[:, kt, :], et[:, 0:w], start=(kt == 0), stop=(kt == nkt - 1))],
                                        reads=[r_et, r_v], writes=[r_po])
                                    S.pe([lambda e, et=et, kt=kt, pr=pr, w=w: e.matmul(
                                        pr[:, 0:w], onesb[:], et[:, 0:w], start=(kt == 0), stop=(kt == nkt - 1))],
                                        reads=[r_et, r_onesb], writes=[r_pr])
                                LOOK = 2
                                for kt in range(min(LOOK, nkt)):
                                    score(kt)
                                for kt in range(nkt):
                                    if kt + LOOK < nkt:
                                        score(kt + LOOK)
                                    pv(kt)
                                S.op("vector", lambda e, rc=rc, pr=pr, w=w: e.reciprocal(rc[:, 0:w], pr[:, 0:w]), reads=[r_pr], writes=[r_rc])
                                finish(po, r_po, rc, r_rc, q0, w)

                        seqs = [(256 * i, 256, 256 * i, 256) for i in range(4)] + [(1024, 1024, 1024, 1280)]
                        for (tq0, T, k0, Sk) in seqs:
                            nkt = Sk // 128

                            def load_k(row0, d):
                                kt_, r_k = kts[cnt["k"] % NB]
                                cnt["k"] += 1
                                S.dma("sync", kt_[0:d, 0:Sk], KT[row0:row0 + d, k0:k0 + Sk], writes=[r_k])
                                return kt_, r_k

                            def load_v(col0):
                                vt_, r_v = vts[cnt["v"] % NB]
                                cnt["v"] += 1
                                S.dma("sync", vt_[:, 0:nkt, :], VV[k0:k0 + Sk, col0:col0 + 128].rearrange("(kt p) c -> p kt c", p=128),
                                      writes=[r_v])
                                return vt_, r_v

                            def load_q(row0, d):
                                qt_, r_q = qts[cnt["q"] % NB]
                                cnt["q"] += 1
                                S.dma("sync", qt_[0:d, 0:T], QT[row0:row0 + d, tq0:tq0 + T], writes=[r_q])
                                return qt_, r_q

                            def fin_plain(chunk, tq0=tq0):
                                def f(po, r_po, rc, r_rc, q0, w):
                                    S.op("vector", lambda e: e.tensor_tensor(
                                        OT[:, chunk, tq0 + q0:tq0 + q0 + w], po[:, 0:w], rc[:, 0:w], ALU.mult),
                                        reads=[r_po, r_rc], writes=[r_OT[chunk]])
                                return f
                            for h in range(8):
                                kt_, r_k = load_k(h * 128, 128)
                                qt_, r_q = load_q(h * 128, 128)
                                vt_, r_v = load_v(h * 128)

                                def fin0(po, r_po, rc, r_rc, q0, w):
                                    S.op("vector", lambda e: e.tensor_tensor(o0s[:, q0 % 512:q0 % 512 + w], po[:, 0:w], rc[:, 0:w], ALU.mult),
                                         reads=[r_po, r_rc], writes=[r_o0])
                                for q0 in range(0, T, 512):
                                    w = min(512, T - q0)

                                    def run_map(c, fin):
                                        qs = qt_[c * 64:(c + 1) * 64, q0:q0 + w]
                                        ks = kt_[c * 64:(c + 1) * 64, :]
                                        attend([(qs, 64)], [(ks, 64)], vt_, r_v, nkt, w, 64 ** -0.5,
                                               lambda po, r_po, rc, r_rc, _q, _w: fin(po, r_po, rc, r_rc, 0, _w), [r_k, r_q])
                                    run_map(0, fin0)

                                    def fin1(po, r_po, rc, r_rc, _q, w, h=h, q0=q0, tq0=tq0):
                                        S.op("vector", lambda e: e.tensor_tensor(o1s[:, 0:w], po[:, 0:w], rc[:, 0:w], ALU.mult),
                                             reads=[r_po, r_rc], writes=[r_o1])
                                        S.op("vector", lambda e: e.scalar_tensor_tensor(
                                            o0s[:, 0:w], o1s[:, 0:w], neglam, o0s[:, 0:w], ALU.mult, ALU.add),
                                            reads=[r_o0, r_o1, r_lam], writes=[r_o0])
                                        S.op("scalar", lambda e: e.activation(out=sqb[:, 0:w], in_=o0s[:, 0:w], func=AF.Square),
                                             reads=[r_o0], writes=[r_sqb])
                                        S.pe([lambda e: e.matmul(psN[:, 0:w], onesb[:], sqb[:, 0:w], start=True, stop=True)],
                                             reads=[r_sqb, r_onesb], writes=[r_psN])
                                        S.op("scalar", lambda e: e.activation(out=o1s[:, 0:w], in_=psN[:, 0:w], func=AF.Sqrt,
                                                                              bias=epsc[:, 0:1], scale=1.0 / 128),
                                             reads=[r_psN, r_epsc], writes=[r_o1])
                                        S.op("vector", lambda e: e.reciprocal(o1s[:, 0:w], o1s[:, 0:w]), reads=[r_o1], writes=[r_o1])
                                        S.op("vector", lambda e: e.scalar_tensor_tensor(
                                            OT[:, h, tq0 + q0:tq0 + q0 + w], o0s[:, 0:w], sgn[:, 0:1], o1s[:, 0:w], ALU.mult, ALU.mult),
                                            reads=[r_o0, r_o1, r_sgn], writes=[r_OT[h]])
                                    run_map(1, fin1)
                            for kvh in range(4):
                                kt_, r_k = load_k(1024 + kvh * 128, 128)
                                vt_, r_v = load_v(1024 + kvh * 128)
                                for gi in range(3):
                                    hq = kvh * 3 + gi
                                    qt_, r_q = load_q(1024 + hq * 128, 128)
                                    attend([(qt_, 128)], [(kt_, 128)], vt_, r_v, nkt, T, 128 ** -0.5, fin_plain(8 + hq), [r_k, r_q])
                            S.dma("sync", kpes[:, 0:Sk], KT[3072:3136, k0:k0 + Sk], writes=[r_kpes])
                            for h in range(12):
                                kt_, r_k = load_k(1536 + h * 128, 128)
                                vt_, r_v = load_v(1536 + h * 128)
                                qt_, r_q = load_q(2560 + h * 192, 128)
                                qp_, r_qp = qpes[h % NB]
                                S.dma("sync", qp_[:, 0:T], QT[2560 + h * 192 + 128:2560 + h * 192 + 192, tq0:tq0 + T], writes=[r_qp])
                                attend([(qt_, 128), (qp_, 64)], [(kt_, 128), (kpes, 64)], vt_, r_v, nkt, T, 192 ** -0.5,
                                       fin_plain(20 + h), [r_k, r_q, r_qp, r_kpes])
                    wov = w_out[l].rearrange("(kc p) n -> p kc n", p=128)
                    wps = [ph.sb("owp%d" % i, [128, KC, 256], BF16) for i in range(2)]
                    pss = [ph.ps("ops%d" % i, [128, 512]) for i in range(4)]
                    xbs = [ph.sb("oxb%d" % i, [128, 512]) for i in range(4)]
                    n = 0
                    for pn in range(D // 256):
                        wp, r_wp = wps[pn % 2]
                        for s4 in range(4):
                            S.dma("gpsimd", wp[:, s4 * 8:(s4 + 1) * 8, :], wov[:, s4 * 8:(s4 + 1) * 8, pn * 256:(pn + 1) * 256], writes=[r_wp])
                        for db in range(2):
                            ch = pn * 2 + db
                            for tb in range(4):
                                g = tb // 2
                                pst, r_ps = pss[n % 4]
                                xb, r_xb = xbs[n % 4]
                                n += 1
                                S.dma("sync", xb[:], XT3[:, ch, tb * 512:(tb + 1) * 512], writes=[r_xb])
                                S.pe([lambda e, pst=pst, wp=wp, kc=kc, db=db, tb=tb: e.matmul(
                                    pst[:], wp[:, kc, db * 128:(db + 1) * 128], OT[:, kc, tb * 512:(tb + 1) * 512],
                                    start=(kc == 0), stop=(kc == KC - 1)) for kc in range(KC)],
                                    reads=[r_wp] + r_OT, writes=[r_ps])
                                S.op("vector", lambda e, xb=xb, pst=pst, ch=ch, g=g: e.scalar_tensor_tensor(
                                    xb[:], pst[:], gT[:, 1, ch, g:g + 1], xb[:], ALU.mult, ALU.add),
                                    reads=[r_ps, r_xb, r_gT], writes=[r_xb])
                                S.dma("sync", XT3[:, ch, tb * 512:(tb + 1) * 512], xb[:], reads=[r_xb])

            stop = debug or "all"
            done = False
            mod_setup(0)
            with Phase(S) as ph0:
                for _ in mod_gen(ph0, 0, 0, 3 * KC, [0]):
                    pass
            for l in range(depth):
                phase_ffn(l, 0, hosted=(l, 3 * KC, NMOD * KC, [1, 2]))
                if stop == "ffn1_%d" % l:
                    done = True
                    break
                phase_mixer(l)
                if stop == "mix_%d" % l:
                    done = True
                    break
                if l + 1 < depth:
                    mod_setup(l + 1)
                    phase_ffn(l, 1, hosted=(l + 1, 0, 3 * KC, [0]))
                else:
                    phase_ffn(l, 1)

            if debug:
                with Phase(S) as ph:
                    bufs = [ph.sb("dbb%d" % i, [128, KC, 128]) for i in range(2)]
                    dbg3 = dbg.rearrange("(kc p) t -> p kc t", p=128)
                    for tt in range(NTILE):
                        b, r_b = bufs[tt % 2]
                        S.dma("sync", b[:], XT3[:, :, tt * 128:(tt + 1) * 128], writes=[r_b])
                        S.dma("sync", dbg3[:, :, tt * 128:(tt + 1) * 128], b[:], reads=[r_b])

            with Phase(S) as ph:
                fnT, r_fnT = ph.sb("fnT", [128, KC])
                psA, r_psA = ph.ps("fpsA", [128, 512])
                load_vec_T(ph, final_norm, KC, fnT, r_fnT, psA, r_psA)
                xts = [ph.sb("fxt%d" % i, [128, KC, 128]) for i in range(2)]
                sqs = [ph.sb("fsq%d" % i, [128, KC, 128], BF16) for i in range(2)]
                rss = [ph.sb("frs%d" % i, [128, 128]) for i in range(2)]
                yts = [ph.sb("fyt%d" % i, [128, D]) for i in range(2)]
                pss = [ph.ps("fps%d" % i, [128, 512]) for i in range(2)]
                pst4 = [ph.ps("fpt%d" % i, [128, 512]) for i in range(4)]
                for tt in range(NTILE):
                    g, lt = tt // 8, tt % 8
                    xt, r_xt = xts[tt % 2]; sq, r_sq = sqs[tt % 2]; rs, r_rs = rss[tt % 2]
                    yt, r_yt = yts[tt % 2]; pst, r_ps = pss[tt % 2]
                    S.dma("sync", xt[:], XT3[:, :, tt * 128:(tt + 1) * 128], writes=[r_xt])
                    S.op("scalar", lambda e, sq=sq, xt=xt: e.activation(out=sq[:], in_=xt[:], func=AF.Square), reads=[r_xt], writes=[r_sq])
                    S.pe([lambda e, pst=pst, sq=sq, kc=kc: e.matmul(pst[:, 0:128], onesb[:], sq[:, kc, :],
                                                                    start=(kc == 0), stop=(kc == KC - 1)) for kc in range(KC)],
                         reads=[r_sq, r_onesb], writes=[r_ps])
                    S.op("scalar", lambda e, rs=rs, pst=pst: e.activation(out=rs[:], in_=pst[:, 0:128], func=AF.Sqrt,
                                                                          bias=epsc[:, 0:1], scale=1.0 / D),
                         reads=[r_ps, r_epsc], writes=[r_rs])
                    S.op("vector", lambda e, rs=rs: e.reciprocal(rs[:], rs[:]), reads=[r_rs], writes=[r_rs])
                    S.op("vector", lambda e, xt=xt, rs=rs: e.tensor_tensor(
                        xt[:], xt[:], rs[:].rearrange("p (o t) -> p o t", o=1).broadcast_to([128, KC, 128]), ALU.mult),
                        reads=[r_xt, r_rs], writes=[r_xt])
                    S.op("gpsimd", lambda e, xt=xt: e.tensor_tensor(
                        xt[:], xt[:], fnT[:].rearrange("p (k o) -> p k o", o=1).broadcast_to([128, KC, 128]), ALU.mult),
                        reads=[r_xt, r_fnT], writes=[r_xt])
                    for k4 in range(8):
                        p4, r_p4 = pst4[k4 % 4]
                        S.pe([lambda e, p4=p4, xt=xt, i=i, kc=k4 * 4 + i: e.matmul(
                            p4[:, i * 128:(i + 1) * 128], xt[:, kc, :], ident[:], start=True, stop=True) for i in range(4)],
                            reads=[r_xt, r_ident], writes=[r_p4])
                        if k4 % 2 == 0:
                            S.op("vector", lambda e, yt=yt, p4=p4, k4=k4: e.tensor_copy(yt[:, k4 * 512:(k4 + 1) * 512], p4[:]),
                                 reads=[r_p4], writes=[r_yt])
                        else:
                            S.op("scalar", lambda e, yt=yt, p4=p4, k4=k4: e.activation(out=yt[:, k4 * 512:(k4 + 1) * 512], in_=p4[:], func=AF.Copy),
                                 reads=[r_p4], writes=[r_yt])
                    S.dma("sync", yout[g][lt * 128:(lt + 1) * 128, :], yt[:], reads=[r_yt])
        S.emit()
    return nc


def rope_tables():
    T = 1024
    t = np.arange(T)
    rows = (t // 64).astype(np.float32)
    cols = (t % 64).astype(np.float32)
    out = {}
    for dh in (64, 128):
        half = dh // 2
        inv = (np.float32(10000.0) ** (-(np.arange(0, half, 2, dtype=np.float32) / np.float32(half)))).astype(np.float32)
        ar = (rows[:, None] * inv[None, :]).astype(np.float32)
        ac = (cols[:, None] * inv[None, :]).astype(np.float32)
        tab = np.concatenate([np.cos(ar), np.cos(ac), np.sin(ar), np.sin(ac)], axis=1).astype(np.float32)
        out[dh] = np.ascontiguousarray(tab)
    return out


_CACHE = {}


def make_in_maps(inputs, debug=None):
    f = lambda a: np.ascontiguousarray(np.asarray(a, dtype=np.float32))
    tabs = rope_tables()
    shared = {k: f(inputs[k]) for k in (
        "w_mod", "b_mod", "norm_ffn1", "norm_mix", "norm_ffn2", "ffn1_w_in", "ffn1_w_out", "ffn2_w_in", "ffn2_w_out",
        "w_in", "w_out", "diff_lq1", "diff_lk1", "diff_lq2", "diff_lk2", "diff_subln", "gqa_q_norm", "gqa_k_norm",
        "mla_q_norm", "mla_kv_norm", "mla_w_uq", "mla_w_ukv", "final_norm")}
    shared["ident"] = np.eye(128, dtype=np.float32)
    shared["cs64"] = tabs[64]
    shared["cs128"] = tabs[128]
    xp = f(inputs["x_prompt"]); xs = f(inputs["x_sample"])
    c = f(inputs["c"]); c_ctx = f(inputs["c_ctx"])
    maps = []
    for i in range(NCORES):
        m = dict(shared)
        m["xp"] = xp[4 * i:4 * i + 4].reshape(1024, D)
        m["xs"] = xs[i].reshape(1024, D)
        m["cdk"] = f(inputs["cache_diff_k"][i]).reshape(2, 256, 1024)
        m["cdv"] = f(inputs["cache_diff_v"][i]).reshape(2, 256, 1024)
        m["cgk"] = f(inputs["cache_gqa_k"][i]).reshape(2, 256, 512)
        m["cgv"] = f(inputs["cache_gqa_v"][i]).reshape(2, 256, 512)
        m["cckv"] = f(inputs["cache_mla_ckv"][i]).reshape(2, 256, 512)
        m["ckpe"] = f(inputs["cache_mla_kpe"][i]).reshape(2, 256, 64)
        m["cvec"] = np.stack([c_ctx, c[i]], axis=0)
        maps.append(m)
    return maps


def kernel(**inputs):
    if "nc" not in _CACHE:
        _CACHE["nc"] = build_program()
    nc = _CACHE["nc"]
    maps = make_in_maps(inputs)
    res = run_bass_kernel_spmd(nc, maps, core_ids=list(range(NCORES)))
    R = res.results
    y_prompt = np.concatenate([R[i]["yp"].reshape(4, 256, D) for i in range(NCORES)], axis=0)
    y_sample = np.stack([R[i]["ys"].reshape(1024, D) for i in range(NCORES)], axis=0)
    ndk = np.concatenate([R[i]["ndk"].reshape(4, 2, 256, 8, 2, 64) for i in range(NCORES)], axis=0)
    ndv = np.concatenate([R[i]["ndv"].reshape(4, 2, 256, 8, 128) for i in range(NCORES)], axis=0)
    ngk = np.concatenate([R[i]["ngk"].reshape(4, 2, 256, 4, 128) for i in range(NCORES)], axis=0)
    ngv = np.concatenate([R[i]["ngv"].reshape(4, 2, 256, 4, 128) for i in range(NCORES)], axis=0)
    nckv = np.concatenate([R[i]["nckv"].reshape(4, 2, 256, 512) for i in range(NCORES)], axis=0)
    nkpe = np.concatenate([R[i]["nkpe"].reshape(4, 2, 256, 64) for i in range(NCORES)], axis=0)
    return (y_prompt.astype(np.float32), y_sample.astype(np.float32), ndk, ndv, ngk, ngv, nckv, nkpe)
```

```python
import math
import contextlib
import numpy as np
import concourse.bass as bass
import concourse.mybir as mybir
from concourse.bass_utils import run_bass_kernel_spmd

F32 = mybir.dt.float32
BF16 = mybir.dt.bfloat16
AF = mybir.ActivationFunctionType
ALU = mybir.AluOpType
AX = mybir.AxisListType

D = 4096
KC = 32
DFF = 11008
FC = 86
NT = 2048
NTILE = 16
NMOD = 9
INC = 7232
EPS = 1e-6
NCORES = 8
QW = 4864
KW = 3136
VW = 3072
KVROWS = 2304
ENGS = ["sync", "scalar", "vector", "gpsimd", "tensor"]
COMPUTE = ["scalar", "vector", "gpsimd", "tensor"]


class Res:
    __slots__ = ("name", "w", "r", "dsem")

    def __init__(self, name):
        self.name = name
        self.w = None
        self.r = []
        self.dsem = None


class SemRec:
    __slots__ = ("h", "count")

    def __init__(self, h):
        self.h = h
        self.count = 0


class Sched:
    def __init__(self, nc, es, n_dma_sems=90):
        self.nc = nc
        self.q = {e: [] for e in ENGS}
        self.esem = {e: SemRec(es.enter_context(nc.semaphore("tk_" + e))) for e in COMPUTE}
        self.seen = {e: {} for e in ENGS}
        self.pool = [SemRec(es.enter_context(nc.semaphore("dm%d" % i))) for i in range(n_dma_sems)]
        self.all_dma = list(self.pool)
        self.uid = 0

    def _deps(self, reads, writes):
        deps = []
        for r in reads:
            if r.w is not None:
                deps.append(r.w)
        for w in writes:
            if w.w is not None:
                deps.append(w.w)
            deps.extend(w.r)
        return deps

    def _waits(self, eng, deps):
        need = {}
        seen = self.seen[eng]
        for d in deps:
            if d[0] == "e":
                _, pe, val = d
                if pe == "tensor" and eng == "tensor":
                    continue
                sr = self.esem[pe]
            else:
                sr = d[1]
                val = sr.count
            k = id(sr)
            if val <= seen.get(k, 0):
                continue
            if k not in need or need[k][1] < val:
                need[k] = (sr, val)
        out = []
        for k, (sr, val) in need.items():
            seen[k] = val
            out.append((sr.h, val))
        return out

    def _mark(self, dep, reads, writes):
        for r in reads:
            r.r.append(dep)
        for w in writes:
            w.w = dep
            w.r = []

    def op(self, eng, fn, reads=(), writes=()):
        waits = self._waits(eng, self._deps(reads, writes))
        sr = self.esem[eng]
        sr.count += 1
        self.q[eng].append((fn, waits, (sr.h, 1)))
        self._mark(("e", eng, sr.count), reads, writes)

    def pe(self, fns, reads=(), writes=()):
        waits = self._waits("tensor", self._deps(reads, writes))
        sr = self.esem["tensor"]
        sr.count += 1
        n = len(fns)
        for i, fn in enumerate(fns):
            self.q["tensor"].append((fn, waits if i == 0 else (), (sr.h, 1) if i == n - 1 else None))
        self._mark(("e", "tensor", sr.count), reads, writes)

    def dma(self, queue, out, in_, reads=(), writes=(), **kw):
        owner = writes[0] if writes else reads[0]
        if owner.dsem is None:
            owner.dsem = self.pool.pop()
        sr = owner.dsem
        waits = self._waits(queue, self._deps(reads, writes))
        sr.count += 16
        self.q[queue].append((lambda e: e.dma_start(out=out, in_=in_, **kw), waits, (sr.h, 16)))
        self._mark(("d", sr), reads, writes)

    def release(self, res_list):
        for r in res_list:
            if r.dsem is not None:
                self.pool.append(r.dsem)
                r.dsem = None

    def barrier(self):
        for eng in ENGS:
            waits = []
            seen = self.seen[eng]
            for pe in COMPUTE:
                sr = self.esem[pe]
                if sr.count > seen.get(id(sr), 0):
                    seen[id(sr)] = sr.count
                    waits.append((sr.h, sr.count))
            for sr in self.all_dma:
                if sr.count > seen.get(id(sr), 0):
                    seen[id(sr)] = sr.count
                    waits.append((sr.h, sr.count))
            if waits:
                self.q[eng].append((None, waits, None))

    def emit(self):
        nc = self.nc
        with nc.Block() as block:
            for eng in ENGS:
                ops = self.q[eng]
                if not ops:
                    continue

                def body(e, ops=ops):
                    for fn, waits, inc in ops:
                        for (sem, val) in waits:
                            e.wait_ge(sem, val)
                        if fn is None:
                            continue
                        ins = fn(e)
                        if inc is not None:
                            ins.then_inc(inc[0], inc[1])

                getattr(block, eng)(body)


class Phase:
    def __init__(self, S):
        self.S = S
        self.es = contextlib.ExitStack()
        self.res = []

    def __enter__(self):
        self.es.__enter__()
        return self

    def sb(self, name, shape, dt=F32, nres=1):
        self.S.uid += 1
        t = self.es.enter_context(self.S.nc.sbuf_tensor("%s_%d" % (name, self.S.uid), shape, dt))
        rs = [Res(name + str(i)) for i in range(nres)]
        self.res.extend(rs)
        return (t, rs[0]) if nres == 1 else (t, rs)

    def ps(self, name, shape, dt=F32):
        self.S.uid += 1
        t = self.es.enter_context(self.S.nc.psum_tensor("%s_%d" % (name, self.S.uid), shape, dt))
        r = Res(name)
        self.res.append(r)
        return t, r

    def __exit__(self, *a):
        self.S.barrier()
        self.S.release(self.res)
        return self.es.__exit__(*a)


def lam_init(l):
    return 0.8 - 0.6 * math.exp(-0.3 * l)


def build_program(depth=2, debug=None):
    nc = bass.Bass("TRN2", target_bir_lowering=False)
    dt = nc.dram_tensor

    def din(name, shape, dtype=F32):
        return dt(name, shape, dtype, kind="ExternalInput").ap()

    def dout(name, shape, dtype=F32):
        return dt(name, shape, dtype, kind="ExternalOutput").ap()

    xin = [din("xp", [1024, D]), din("xs", [1024, D])]
    cdk = din("cdk", [2, 256, 1024]); cdv = din("cdv", [2, 256, 1024])
    cgk = din("cgk", [2, 256, 512]); cgv = din("cgv", [2, 256, 512])
    cckv = din("cckv", [2, 256, 512]); ckpe = din("ckpe", [2, 256, 64])
    cvec = din("cvec", [2, D])
    w_mod = din("w_mod", [2, D, NMOD * D]); b_mod = din("b_mod", [2, NMOD * D])
    norm_ffn1 = din("norm_ffn1", [2, D]); norm_mix = din("norm_mix", [2, D]); norm_ffn2 = din("norm_ffn2", [2, D])
    ffn_w_in = [din("ffn1_w_in", [2, D, 2 * DFF]), din("ffn2_w_in", [2, D, 2 * DFF])]
    ffn_w_out = [din("ffn1_w_out", [2, DFF, D]), din("ffn2_w_out", [2, DFF, D])]
    w_in = din("w_in", [2, D, INC]); w_out = din("w_out", [2, D, D])
    lq1 = din("diff_lq1", [2, 64]); lk1 = din("diff_lk1", [2, 64])
    lq2 = din("diff_lq2", [2, 64]); lk2 = din("diff_lk2", [2, 64])
    subln = din("diff_subln", [2, 128])
    gqn = din("gqa_q_norm", [2, 128]); gkn = din("gqa_k_norm", [2, 128])
    mqn = din("mla_q_norm", [2, 1024]); mkvn = din("mla_kv_norm", [2, 512])
    w_uq = din("mla_w_uq", [2, 1024, 2304]); w_ukv = din("mla_w_ukv", [2, 512, 3072])
    final_norm = din("final_norm", [D])
    ident_d = din("ident", [128, 128])
    cs64_d = din("cs64", [1024, 64]); cs128_d = din("cs128", [1024, 128])

    yout = [dout("yp", [1024, D]), dout("ys", [1024, D])]
    ndk = dout("ndk", [4, 2, 256, 1024]); ndv = dout("ndv", [4, 2, 256, 1024])
    ngk = dout("ngk", [4, 2, 256, 512]); ngv = dout("ngv", [4, 2, 256, 512])
    nckv = dout("nckv", [4, 2, 256, 512]); nkpe = dout("nkpe", [4, 2, 256, 64])

    XT = dt("XT", [D, NT], F32).ap()
    HT = dt("HT", [DFF, NT], BF16).ap()
    RAW = dt("RAW", [NT, INC], F32).ap()
    QT = dt("QT", [QW, NT], BF16).ap()
    KT = dt("KT", [KW, KVROWS], BF16).ap()
    VV = dt("VV", [KVROWS, VW], BF16).ap()
    dbg = None
    if debug:
        dbg = dout("dbg", [D, NT])

    XT3 = XT.rearrange("(kc p) t -> p kc t", p=128)

    with contextlib.ExitStack() as es:
        S = Sched(nc, es)
        with Phase(S) as G:
            ident, r_ident = G.sb("ident", [128, 128])
            identb, r_identb = G.sb("identb", [128, 128], BF16)
            onesb, r_onesb = G.sb("onesb", [128, 128], BF16)
            epsc, r_epsc = G.sb("epsc", [128, 1])
            modT, r_modT = G.sb("modT", [128, NMOD * KC, 2])
            coefA, r_coefA = G.sb("coefA", [128, 3, KC, 2])
            gT, r_gT = G.sb("gT", [128, 3, KC, 2])
            cv32, r_cv32 = G.sb("cv32", [32, 2, 128])
            cvT, r_cvT = G.sb("cvT", [128, KC, 2])
            scb, r_scb = G.sb("scb", [128, KC, 2], BF16)
            bmT, r_bmT = G.sb("bmT", [128, NMOD * KC])
            nrmT, r_nrmT = G.sb("nrmT", [128, 3, KC])
            S.dma("sync", ident[:], ident_d[:, :], writes=[r_ident])
            S.op("vector", lambda e: e.tensor_copy(identb[:], ident[:]), reads=[r_ident], writes=[r_identb])
            S.op("vector", lambda e: e.memset(onesb[:], 1.0), writes=[r_onesb])
            S.op("vector", lambda e: e.memset(epsc[:], EPS), writes=[r_epsc])

            def load_vec_T(ph, vec_ap, m, out_ap, out_res, pstile, psres):
                j0 = 0
                while j0 < m:
                    mm = min(96, m - j0)
                    tmp, r_tmp = ph.sb("lvt", [96, 128])
                    S.dma("sync", tmp[0:mm, :], vec_ap[j0 * 128:(j0 + mm) * 128].rearrange("(j p) -> j p", p=128),
                          writes=[r_tmp])
                    S.pe([lambda e, tmp=tmp, mm=mm: e.matmul(pstile[:, 0:mm], tmp[0:mm, :], ident[0:mm, 0:mm],
                                                             start=True, stop=True)],
                         reads=[r_tmp, r_ident], writes=[psres])
                    S.op("vector", lambda e, j0=j0, mm=mm: e.tensor_copy(out_ap[:, j0:j0 + mm], pstile[:, 0:mm]),
                         reads=[psres], writes=[out_res])
                    j0 += mm

            with Phase(S) as ph:
                pss = [ph.ps("ips%d" % i, [128, 512]) for i in range(4)]
                xts = [ph.sb("ixt%d" % i, [128, KC, 128]) for i in range(2)]
                xls = [ph.sb("ixl%d" % i, [128, D]) for i in range(2)]
                for tt in range(NTILE):
                    g, lt = tt // 8, tt % 8
                    xl, r_xl = xls[tt % 2]
                    xt, r_xt = xts[tt % 2]
                    S.dma("sync", xl[:], xin[g][lt * 128:(lt + 1) * 128, :], writes=[r_xl])
                    for k4 in range(8):
                        pst, r_ps = pss[k4 % 4]
                        S.pe([lambda e, pst=pst, xl=xl, kc=k4 * 4 + i, i=i: e.matmul(
                            pst[:, i * 128:(i + 1) * 128], xl[:, kc * 128:(kc + 1) * 128], ident[:],
                            start=True, stop=True) for i in range(4)],
                            reads=[r_xl, r_ident], writes=[r_ps])
                        eng = "vector" if k4 % 2 == 0 else "scalar"
                        if eng == "vector":
                            S.op("vector", lambda e, xt=xt, pst=pst, k4=k4: e.tensor_copy(
                                xt[:, k4 * 4:(k4 + 1) * 4, :], pst[:].rearrange("p (a b) -> p a b", a=4)),
                                reads=[r_ps], writes=[r_xt])
                        else:
                            S.op("scalar", lambda e, xt=xt, pst=pst, k4=k4: e.activation(
                                out=xt[:, k4 * 4:(k4 + 1) * 4, :], in_=pst[:].rearrange("p (a b) -> p a b", a=4),
                                func=AF.Copy), reads=[r_ps], writes=[r_xt])
                    S.dma("sync", XT3[:, :, tt * 128:(tt + 1) * 128], xt[:], reads=[r_xt])

            def mod_setup(l):
                with Phase(S) as ph:
                    psA, r_psA = ph.ps("mpsA", [128, 512])
                    for g in range(2):
                        S.dma("sync", cv32[:, g, :], cvec[g, :].rearrange("(j p) -> j p", p=128), writes=[r_cv32])
                    for g in range(2):
                        S.pe([lambda e, g=g: e.matmul(psA[:, g * 32:(g + 1) * 32], cv32[:, g, :], ident[0:32, 0:32],
                                                      start=True, stop=True)], reads=[r_cv32, r_ident], writes=[r_psA])
                    S.op("vector", lambda e: e.tensor_copy(cvT[:].rearrange("p k g -> p g k"),
                                                           psA[:, 0:64].rearrange("p (g k) -> p g k", g=2)),
                         reads=[r_psA], writes=[r_cvT])
                    S.op("scalar", lambda e: e.activation(out=scb[:], in_=cvT[:], func=AF.Silu),
                         reads=[r_cvT], writes=[r_scb])
                    load_vec_T(ph, b_mod[l, :], NMOD * KC, bmT, r_bmT, psA, r_psA)
                    for i, nv in enumerate((norm_ffn1, norm_mix, norm_ffn2)):
                        load_vec_T(ph, nv[l, :], KC, nrmT[:, i, :], r_nrmT, psA, r_psA)

            def mod_gen(ph, l, j_lo, j_hi, parts):
                wps = [ph.sb("mwp%d" % i, [128, KC, 128], BF16) for i in range(2)]
                psB = [ph.ps("mpsB%d" % i, [128, 512]) for i in range(2)]
                wv = w_mod[l].rearrange("(kc p) n -> p kc n", p=128)
                for n_, j in enumerate(range(j_lo, j_hi)):
                    wp, r_wp = wps[n_ % 2]
                    pst, r_ps = psB[n_ % 2]
                    for s2 in range(2):
                        S.dma("gpsimd", wp[:, s2 * 16:(s2 + 1) * 16, :], wv[:, s2 * 16:(s2 + 1) * 16, j * 128:(j + 1) * 128],
                              writes=[r_wp])
                    S.pe([lambda e, pst=pst, wp=wp, kc=kc: e.matmul(pst[:, 0:2], wp[:, kc, :], scb[:, kc, :],
                                                                    start=(kc == 0), stop=(kc == KC - 1)) for kc in range(KC)],
                         reads=[r_wp, r_scb], writes=[r_ps])
                    S.op("vector", lambda e, pst=pst, j=j: e.tensor_tensor(
                        modT[:, j, :], pst[:, 0:2], bmT[:, j:j + 1].broadcast_to([128, 2]), ALU.add),
                        reads=[r_ps, r_bmT], writes=[r_modT])
                    yield
                for i in parts:
                    sc = modT[:, (3 * i + 1) * KC:(3 * i + 2) * KC, :]
                    gg = modT[:, (3 * i + 2) * KC:(3 * i + 3) * KC, :]
                    S.op("vector", lambda e, i=i, sc=sc: e.scalar_tensor_tensor(
                        coefA[:, i, :, :], sc, 1.0,
                        nrmT[:, i, :].rearrange("p (k o) -> p k o", o=1).broadcast_to([128, KC, 2]),
                        ALU.add, ALU.mult), reads=[r_modT, r_nrmT], writes=[r_coefA])
                    S.op("vector", lambda e, i=i, gg=gg: e.tensor_scalar(
                        gT[:, i, :, :], gg, 0.5 if i != 1 else 1.0, None, ALU.mult),
                        reads=[r_modT], writes=[r_gT])

            def phase_norm(ph, AT, r_AT, i_norm):
                with Phase(S) as p2:
                    xts = [p2.sb("nxt%d" % i, [128, KC, 128]) for i in range(2)]
                    sqs = [p2.sb("nsq%d" % i, [128, KC, 128], BF16) for i in range(2)]
                    rss = [p2.sb("nrs%d" % i, [128, 128]) for i in range(2)]
                    pss = [p2.ps("nps%d" % i, [128, 512]) for i in range(2)]
                    for tt in range(NTILE):
                        g = tt // 8
                        xt, r_xt = xts[tt % 2]; sq, r_sq = sqs[tt % 2]; t1, r_t1 = xt, r_xt
                        rs, r_rs = rss[tt % 2]; pst, r_ps = pss[tt % 2]
                        S.dma("sync", xt[:], XT3[:, :, tt * 128:(tt + 1) * 128], writes=[r_xt])
                        S.op("scalar", lambda e, sq=sq, xt=xt: e.activation(out=sq[:], in_=xt[:], func=AF.Square),
                             reads=[r_xt], writes=[r_sq])
                        S.pe([lambda e, pst=pst, sq=sq, kc=kc: e.matmul(pst[:, 0:128], onesb[:], sq[:, kc, :],
                                                                        start=(kc == 0), stop=(kc == KC - 1))
                              for kc in range(KC)], reads=[r_sq, r_onesb], writes=[r_ps])
                        S.op("scalar", lambda e, rs=rs, pst=pst: e.activation(out=rs[:], in_=pst[:, 0:128], func=AF.Sqrt,
                                                                              bias=epsc[:, 0:1], scale=1.0 / D),
                             reads=[r_ps, r_epsc], writes=[r_rs])
                        S.op("vector", lambda e, rs=rs: e.reciprocal(rs[:], rs[:]), reads=[r_rs], writes=[r_rs])
                        S.op("vector", lambda e, t1=t1, xt=xt, rs=rs: e.tensor_tensor(
                            t1[:], xt[:], rs[:].rearrange("p (o t) -> p o t", o=1).broadcast_to([128, KC, 128]), ALU.mult),
                            reads=[r_xt, r_rs], writes=[r_t1])
                        S.op("gpsimd", lambda e, t1=t1, g=g: e.tensor_tensor(
                            t1[:], t1[:], coefA[:, i_norm, :, g:g + 1].broadcast_to([128, KC, 128]), ALU.mult),
                            reads=[r_t1, r_coefA], writes=[r_t1])
                        sh = modT[:, (3 * i_norm) * KC:(3 * i_norm + 1) * KC, g:g + 1]
                        S.op("vector", lambda e, t1=t1, tt=tt, sh=sh: e.tensor_tensor(
                            AT[:, :, tt * 128:(tt + 1) * 128], t1[:], sh.broadcast_to([128, KC, 128]), ALU.add),
                            reads=[r_t1, r_modT], writes=[r_AT[tt]])

            def phase_ffn(l, which, hosted=None):
                i_norm = 0 if which == 0 else 2
                wi = ffn_w_in[which][l].rearrange("(kc p) n -> p kc n", p=128)
                wo = ffn_w_out[which][l]
                HT3 = HT.rearrange("(fc p) t -> p fc t", p=128)
                with Phase(S) as ph:
                    AT, r_AT = ph.sb("AT", [128, KC, NT], BF16, nres=NTILE)
                    phase_norm(ph, AT, r_AT, i_norm)
                    wps = [ph.sb("fwp%d" % i, [128, KC, 256], BF16) for i in range(2)]
                    gen = mod_gen(ph, *hosted) if hosted is not None else None
                    per_blk = 0 if hosted is None else 2
                    psG = [ph.ps("psG%d" % i, [128, 512]) for i in range(2)]
                    psU = [ph.ps("psU%d" % i, [128, 512]) for i in range(2)]
                    sgs = [ph.sb("sg%d" % i, [128, 512]) for i in range(2)]
                    hbs = [ph.sb("hb%d" % i, [128, NT], BF16) for i in range(2)]
                    n = 0
                    for j in range(FC):
                        wp, r_wp = wps[j % 2]
                        for half, c0 in ((0, j * 128), (1, DFF + j * 128)):
                            for s2 in range(2):
                                S.dma("gpsimd", wp[:, s2 * 16:(s2 + 1) * 16, half * 128:(half + 1) * 128],
                                      wi[:, s2 * 16:(s2 + 1) * 16, c0:c0 + 128], writes=[r_wp])
                        hb, r_hb = hbs[j % 2]
                        for tb in range(4):
                            pg, r_pg = psG[n % 2]; pu, r_pu = psU[n % 2]; sg, r_sg = sgs[n % 2]
                            n += 1
                            rd = [r_wp] + r_AT[tb * 4:(tb + 1) * 4]
                            S.pe([lambda e, pg=pg, wp=wp, kc=kc, tb=tb: e.matmul(
                                pg[:], wp[:, kc, 0:128], AT[:, kc, tb * 512:(tb + 1) * 512],
                                start=(kc == 0), stop=(kc == KC - 1)) for kc in range(KC)], reads=rd, writes=[r_pg])
                            S.pe([lambda e, pu=pu, wp=wp, kc=kc, tb=tb: e.matmul(
                                pu[:], wp[:, kc, 128:256], AT[:, kc, tb * 512:(tb + 1) * 512],
                                start=(kc == 0), stop=(kc == KC - 1)) for kc in range(KC)], reads=rd, writes=[r_pu])
                            S.op("scalar", lambda e, sg=sg, pg=pg: e.activation(out=sg[:], in_=pg[:], func=AF.Silu),
                                 reads=[r_pg], writes=[r_sg])
                            S.op("vector", lambda e, hb=hb, sg=sg, pu=pu, tb=tb: e.tensor_tensor(
                                hb[:, tb * 512:(tb + 1) * 512], sg[:], pu[:], ALU.mult),
                                reads=[r_sg, r_pu], writes=[r_hb])
                        S.dma("sync", HT3[:, j, :], hb[:], reads=[r_hb])
                        if gen is not None:
                            for _ in range(per_blk):
                                if next(gen, "done") == "done":
                                    gen = None
                                    break
                    if gen is not None:
                        for _ in gen:
                            pass
                with Phase(S) as ph:
                    KP = 8
                    panels = [(f0, min(KP, FC - f0)) for f0 in range(0, FC, KP)]
                    yacc, r_y = ph.sb("yacc", [128, 8, NT], F32, nres=32)
                    hps = [ph.sb("hp%d" % i, [128, KP, NT], BF16) for i in range(2)]
                    wops = [ph.sb("wop%d" % i, [128, KP, 1024], BF16) for i in range(2)]
                    pss = [ph.ps("yps%d" % i, [128, 512]) for i in range(4)]
                    n = 0
                    for dq in range(4):
                        for db in range(8):
                            ch = dq * 8 + db
                            for tb in range(4):
                                S.dma("sync", yacc[:, db, tb * 512:(tb + 1) * 512], XT3[:, ch, tb * 512:(tb + 1) * 512],
                                      writes=[r_y[db * 4 + tb]])
                        for pi, (f0, kp) in enumerate(panels):
                            hp, r_hp = hps[(dq * len(panels) + pi) % 2]
                            wop, r_wop = wops[(dq * len(panels) + pi) % 2]
                            S.dma("gpsimd", hp[:, 0:kp, :], HT3[:, f0:f0 + kp, :], writes=[r_hp])
                            S.dma("gpsimd", wop[:, 0:kp, :],
                                  wo[f0 * 128:(f0 + kp) * 128, dq * 1024:(dq + 1) * 1024].rearrange("(fc p) n -> p fc n", p=128),
                                  writes=[r_wop])
                            for db in range(8):
                                ch = dq * 8 + db
                                for tb in range(4):
                                    g = tb // 2
                                    pst, r_ps = pss[n % 4]
                                    n += 1
                                    S.pe([lambda e, pst=pst, wop=wop, hp=hp, k=k, kp=kp, db=db, tb=tb: e.matmul(
                                        pst[:], wop[:, k, db * 128:(db + 1) * 128], hp[:, k, tb * 512:(tb + 1) * 512],
                                        start=(k == 0), stop=(k == kp - 1)) for k in range(kp)],
                                        reads=[r_wop, r_hp], writes=[r_ps])
                                    ry = r_y[db * 4 + tb]
                                    ya = yacc[:, db, tb * 512:(tb + 1) * 512]
                                    S.op("vector", lambda e, ya=ya, pst=pst, ch=ch, g=g: e.scalar_tensor_tensor(
                                        ya, pst[:], gT[:, i_norm, ch, g:g + 1], ya, ALU.mult, ALU.add),
                                        reads=[r_ps, ry, r_gT], writes=[ry])
                        for db in range(8):
                            ch = dq * 8 + db
                            for tb in range(4):
                                S.dma("scalar", XT3[:, ch, tb * 512:(tb + 1) * 512], yacc[:, db, tb * 512:(tb + 1) * 512],
                                      reads=[r_y[db * 4 + tb]])

            def rope_ops(eng, xv, ov, ctab, stab, nvec, qd, tmpv, rd, wr, r_tmp):
                cb = ctab.rearrange("p (o a) q -> p o a q", o=1).broadcast_to([128, nvec, 2, qd])
                sb_ = stab.rearrange("p (o a) q -> p o a q", o=1).broadcast_to([128, nvec, 2, qd])
                x0 = xv[:, :, :, 0, :]; x1 = xv[:, :, :, 1, :]
                t0 = tmpv[:, 0]; t1 = tmpv[:, 1]
                S.op(eng, lambda e: e.tensor_tensor(t0, x1, sb_, ALU.mult), reads=rd, writes=[r_tmp])
                S.op(eng, lambda e: e.tensor_tensor(t1, x0, sb_, ALU.mult), reads=rd, writes=[r_tmp])
                S.op(eng, lambda e: e.tensor_tensor(ov[:, :, :, 0, :], x0, cb, ALU.mult), reads=rd, writes=wr)
                S.op(eng, lambda e: e.tensor_tensor(ov[:, :, :, 1, :], x1, cb, ALU.mult), reads=rd, writes=wr)
                S.op(eng, lambda e: e.tensor_tensor(ov[:, :, :, 0, :], ov[:, :, :, 0, :], t0, ALU.subtract),
                     reads=[r_tmp] + wr, writes=wr)
                S.op(eng, lambda e: e.tensor_tensor(ov[:, :, :, 1, :], ov[:, :, :, 1, :], t1, ALU.add),
                     reads=[r_tmp] + wr, writes=wr)

            def phase_mixer(l):
                win = w_in[l].rearrange("(kc p) n -> p kc n", p=128)
                RAW3 = RAW.rearrange("(tt p) c -> p tt c", p=128)
                with Phase(S) as ph:
                    AT, r_AT = ph.sb("AT", [128, KC, NT], BF16, nres=NTILE)
                    phase_norm(ph, AT, r_AT, 1)
                    wps = [ph.sb("iwp%d" % i, [128, KC, 256], BF16) for i in range(2)]
                    sts = [ph.sb("ist%d" % i, [128, NTILE, 256]) for i in range(2)]
                    pss = [ph.ps("ips%d" % i, [128, 512]) for i in range(4)]
                    npan = (INC + 255) // 256
                    n = 0
                    for pn in range(npan):
                        c0 = pn * 256
                        w = min(256, INC - c0)
                        wp, r_wp = wps[pn % 2]
                        st, r_st = sts[pn % 2]
                        for s4 in range(4):
                            S.dma("gpsimd", wp[:, s4 * 8:(s4 + 1) * 8, 0:w], win[:, s4 * 8:(s4 + 1) * 8, c0:c0 + w], writes=[r_wp])
                        for tt in range(NTILE):
                            pst, r_ps = pss[n % 4]
                            n += 1
                            S.pe([lambda e, pst=pst, wp=wp, kc=kc, tt=tt, w=w: e.matmul(
                                pst[:, 0:w], AT[:, kc, tt * 128:(tt + 1) * 128], wp[:, kc, 0:w],
                                start=(kc == 0), stop=(kc == KC - 1)) for kc in range(KC)],
                                reads=[r_wp, r_AT[tt]], writes=[r_ps])
                            if tt % 2 == 0:
                                S.op("vector", lambda e, st=st, pst=pst, tt=tt, w=w: e.tensor_copy(st[:, tt, 0:w], pst[:, 0:w]),
                                     reads=[r_ps], writes=[r_st])
                            else:
                                S.op("scalar", lambda e, st=st, pst=pst, tt=tt, w=w: e.activation(
                                    out=st[:, tt, 0:w], in_=pst[:, 0:w], func=AF.Copy), reads=[r_ps], writes=[r_st])
                        S.dma("sync", RAW3[:, :, c0:c0 + w], st[:, :, 0:w], reads=[r_st])

                with Phase(S) as ph:
                    wuq, r_wuq = ph.sb("wuq", [128, 8, 2304], BF16)
                    wukv, r_wukv = ph.sb("wukv", [128, 4, 3072], BF16)
                    S.dma("gpsimd", wuq[:], w_uq[l].rearrange("(kc p) n -> p kc n", p=128), writes=[r_wuq])
                    S.dma("gpsimd", wukv[:], w_ukv[l].rearrange("(kc p) n -> p kc n", p=128), writes=[r_wukv])
                    gb, r_gb = ph.sb("gb", [128, 2, 128])
                    mqb, r_mqb = ph.sb("mqb", [128, 1024])
                    mkb, r_mkb = ph.sb("mkb", [128, 512])
                    S.dma("sync", gb[:, 0, :], gqn[l, :].partition_broadcast(128), writes=[r_gb])
                    S.dma("sync", gb[:, 1, :], gkn[l, :].partition_broadcast(128), writes=[r_gb])
                    S.dma("sync", mqb[:], mqn[l, :].partition_broadcast(128), writes=[r_mqb])
                    S.dma("sync", mkb[:], mkvn[l, :].partition_broadcast(128), writes=[r_mkb])
                    raws = [ph.sb("raw%d" % i, [128, INC]) for i in range(1)]
                    wk, r_wk = ph.sb("wk", [128, 3584])
                    sq, r_sq = ph.sb("sq", [128, 3584])
                    ss, r_ss = ph.sb("ss", [128, 20])
                    rtmp, r_rtmp = ph.sb("rtmp", [128, 2, 1536])
                    cs64, r_cs64 = ph.sb("cs64", [128, 64])
                    cs128, r_cs128 = ph.sb("cs128", [128, 128])
                    Qp, r_Qp = ph.sb("Qp", [128, QW], BF16)
                    Kp, r_Kp = ph.sb("Kp", [128, KW], BF16)
                    Vp, r_Vp = ph.sb("Vp", [128, VW], BF16)
                    cnb, r_cnb = ph.sb("cnb", [128, 1536], BF16)
                    cnT, r_cnT = ph.sb("cnT", [128, 12, 128], BF16)
                    mqf, r_mqf = ph.sb("mqf", [128, 2304])
                    QTs, r_QTs = ph.sb("QTs", [128, 38, 128], BF16)
                    KTs, r_KTs = ph.sb("KTs", [128, 25, 128], BF16)
                    pT = [ph.ps("pT%d" % i, [128, 1024], BF16) for i in range(2)]
                    pM = [ph.ps("pM%d" % i, [128, 512]) for i in range(3)]
                    QT3 = QT.rearrange("(c p) t -> p c t", p=128)
                    KTm = KT[0:3072, :].rearrange("(c p) t -> p c t", p=128)
                    nT = [0]
                    nM = [0]

                    def transposes(src, r_src, nblk, dst, r_dst, last_w=128):
                        b = 0
                        while b < nblk:
                            nb = min(8, nblk - b)
                            pt, r_pt = pT[nT[0] % 2]
                            nT[0] += 1
                            fns = []
                            for i in range(nb):
                                wd = last_w if (b + i == nblk - 1) else 128
                                fns.append(lambda e, pt=pt, i=i, c=b + i, wd=wd: e.transpose(
                                    pt[0:wd, i * 128:(i + 1) * 128], src[:, c * 128:c * 128 + wd], identb[:]))
                            S.pe(fns, reads=[r_src, r_identb], writes=[r_pt])
                            S.op("scalar" if (nT[0] % 2) else "vector",
                                 (lambda e, pt=pt, b=b, nb=nb: e.activation(
                                     out=dst[:, b:b + nb, :], in_=pt[:, 0:nb * 128].rearrange("p (a t) -> p a t", a=nb), func=AF.Copy))
                                 if (nT[0] % 2) else
                                 (lambda e, pt=pt, b=b, nb=nb: e.tensor_copy(
                                     dst[:, b:b + nb, :], pt[:, 0:nb * 128].rearrange("p (a t) -> p a t", a=nb))),
                                 reads=[r_pt], writes=[r_dst])
                            b += nb

                    def mla_kv(krow0):
                        for nb in range(6):
                            pm, r_pm = pM[nM[0] % 3]
                            nM[0] += 1
                            S.pe([lambda e, pm=pm, kc=kc, nb=nb: e.matmul(
                                pm[:], cnT[:, 8 + kc, :], wukv[:, kc, nb * 512:(nb + 1) * 512],
                                start=(kc == 0), stop=(kc == 3)) for kc in range(4)],
                                reads=[r_cnT, r_wukv], writes=[r_pm])
                            pv = pm[:].rearrange("p (h c) -> p h c", h=2)
                            kdst = Kp[:, 1536 + nb * 256:1536 + (nb + 1) * 256].rearrange("p (h c) -> p h c", h=2)
                            vdst = Vp[:, 1536 + nb * 256:1536 + (nb + 1) * 256].rearrange("p (h c) -> p h c", h=2)
                            if nb % 2 == 0:
                                S.op("vector", lambda e, pv=pv, kdst=kdst: e.tensor_copy(kdst, pv[:, :, 0:128]), reads=[r_pm], writes=[r_Kp])
                                S.op("vector", lambda e, pv=pv, vdst=vdst: e.tensor_copy(vdst, pv[:, :, 128:256]), reads=[r_pm], writes=[r_Vp])
                            else:
                                S.op("scalar", lambda e, pv=pv, kdst=kdst: e.activation(out=kdst, in_=pv[:, :, 0:128], func=AF.Copy),
                                     reads=[r_pm], writes=[r_Kp])
                                S.op("scalar", lambda e, pv=pv, vdst=vdst: e.activation(out=vdst, in_=pv[:, :, 128:256], func=AF.Copy),
                                     reads=[r_pm], writes=[r_Vp])

                    def store_kv(krow0):
                        transposes(Kp, r_Kp, 25, KTs, r_KTs, last_w=64)
                        for c0, c1 in ((0, 8), (8, 16), (16, 24)):
                            S.dma("sync", KTm[:, c0:c1, krow0:krow0 + 128], KTs[:, c0:c1, :], reads=[r_KTs])
                        S.dma("sync", KT[3072:3136, krow0:krow0 + 128], KTs[0:64, 24, :], reads=[r_KTs])
                        S.dma("sync", VV[krow0:krow0 + 128, :], Vp[:], reads=[r_Vp])

                    for ct in range(2):
                        raw, r_raw = raws[0]
                        rows = slice(ct * 128, (ct + 1) * 128)
                        S.dma("sync", raw[:, 0:1024], cdk[l, rows, :], writes=[r_raw])
                        S.dma("sync", raw[:, 1024:1536], cgk[l, rows, :], writes=[r_raw])
                        S.dma("sync", raw[:, 1536:1600], ckpe[l, rows, :], writes=[r_raw])
                        S.dma("sync", raw[:, 2048:3072], cdv[l, rows, :], writes=[r_raw])
                        S.dma("sync", raw[:, 3072:3584], cgv[l, rows, :], writes=[r_raw])
                        S.dma("sync", raw[:, 4096:4608], cckv[l, rows, :], writes=[r_raw])
                        S.op("vector", lambda e, raw=raw: e.tensor_copy(Kp[:, 0:1536], raw[:, 0:1536]), reads=[r_raw], writes=[r_Kp])
                        S.op("vector", lambda e, raw=raw: e.tensor_copy(Kp[:, 3072:3136], raw[:, 1536:1600]), reads=[r_raw], writes=[r_Kp])
                        S.op("scalar", lambda e, raw=raw: e.activation(out=Vp[:, 0:1536], in_=raw[:, 2048:3584], func=AF.Copy),
                             reads=[r_raw], writes=[r_Vp])
                        S.op("vector", lambda e, raw=raw: e.tensor_copy(cnb[:, 1024:1536], raw[:, 4096:4608]), reads=[r_raw], writes=[r_cnb])
                        transposes(cnb[:, 1024:1536], r_cnb, 4, cnT[:, 8:12, :], r_cnT)
                        mla_kv(0)
                        store_kv(1024 + ct * 128)

                    for tt in range(NTILE):
                        g, lt = tt // 8, tt % 8
                        raw, r_raw = raws[0]
                        S.dma("sync", raw[:], RAW[tt * 128:(tt + 1) * 128, :], writes=[r_raw])
                        if g == 1:
                            S.dma("sync", cs64[:], cs64_d[lt * 128:(lt + 1) * 128, :], writes=[r_cs64])
                            S.dma("sync", cs128[:], cs128_d[lt * 128:(lt + 1) * 128, :], writes=[r_cs128])
                        S.op("gpsimd", lambda e, raw=raw: e.tensor_tensor(sq[:, 0:2048], raw[:, 3072:5120], raw[:, 3072:5120], ALU.mult),
                             reads=[r_raw], writes=[r_sq])
                        S.op("gpsimd", lambda e, raw=raw: e.tensor_tensor(sq[:, 2048:3584], raw[:, 5632:7168], raw[:, 5632:7168], ALU.mult),
                             reads=[r_raw], writes=[r_sq])
                        S.op("vector", lambda e: e.tensor_reduce(ss[:, 0:16], sq[:, 0:2048].rearrange("p (h c) -> p h c", h=16), AX.X, ALU.add),
                             reads=[r_sq], writes=[r_ss])
                        S.op("vector", lambda e: e.tensor_reduce(ss[:, 16:17], sq[:, 2048:3072], AX.X, ALU.add), reads=[r_sq], writes=[r_ss])
                        S.op("vector", lambda e: e.tensor_reduce(ss[:, 17:18], sq[:, 3072:3584], AX.X, ALU.add), reads=[r_sq], writes=[r_ss])
                        S.op("scalar", lambda e: e.activation(out=ss[:, 0:16], in_=ss[:, 0:16], func=AF.Sqrt, bias=epsc[:, 0:1], scale=1.0 / 128),
                             reads=[r_ss, r_epsc], writes=[r_ss])
                        S.op("scalar", lambda e: e.activation(out=ss[:, 16:17], in_=ss[:, 16:17], func=AF.Sqrt, bias=epsc[:, 0:1], scale=1.0 / 1024),
                             reads=[r_ss, r_epsc], writes=[r_ss])
                        S.op("scalar", lambda e: e.activation(out=ss[:, 17:18], in_=ss[:, 17:18], func=AF.Sqrt, bias=epsc[:, 0:1], scale=1.0 / 512),
                             reads=[r_ss, r_epsc], writes=[r_ss])
                        S.op("vector", lambda e: e.reciprocal(ss[:, 0:18], ss[:, 0:18]), reads=[r_ss], writes=[r_ss])
                        S.op("vector", lambda e, raw=raw: e.tensor_tensor(
                            wk[:, 0:2048].rearrange("p (h c) -> p h c", h=16), raw[:, 3072:5120].rearrange("p (h c) -> p h c", h=16),
                            ss[:, 0:16].rearrange("p (h o) -> p h o", o=1).broadcast_to([128, 16, 128]), ALU.mult),
                            reads=[r_raw, r_ss], writes=[r_wk])
                        S.op("gpsimd", lambda e: e.tensor_tensor(
                            wk[:, 0:1536].rearrange("p (h c) -> p h c", h=12), wk[:, 0:1536].rearrange("p (h c) -> p h c", h=12),
                            gb[:, 0:1, :].broadcast_to([128, 12, 128]), ALU.mult), reads=[r_wk, r_gb], writes=[r_wk])
                        S.op("gpsimd", lambda e: e.tensor_tensor(
                            wk[:, 1536:2048].rearrange("p (h c) -> p h c", h=4), wk[:, 1536:2048].rearrange("p (h c) -> p h c", h=4),
                            gb[:, 1:2, :].broadcast_to([128, 4, 128]), ALU.mult), reads=[r_wk, r_gb], writes=[r_wk])
                        S.op("vector", lambda e, raw=raw: e.scalar_tensor_tensor(
                            wk[:, 2048:3072], raw[:, 5632:6656], ss[:, 16:17], mqb[:], ALU.mult, ALU.mult),
                            reads=[r_raw, r_ss, r_mqb], writes=[r_wk])
                        S.op("vector", lambda e, raw=raw: e.scalar_tensor_tensor(
                            wk[:, 3072:3584], raw[:, 6656:7168], ss[:, 17:18], mkb[:], ALU.mult, ALU.mult),
                            reads=[r_raw, r_ss, r_mkb], writes=[r_wk])
                        S.op("scalar", lambda e: e.activation(out=cnb[:], in_=wk[:, 2048:3584], func=AF.Copy), reads=[r_wk], writes=[r_cnb])
                        if g == 0:
                            b, t0 = lt // 2, (lt % 2) * 128
                            S.dma("sync", ndk[b, l, t0:t0 + 128, :], raw[:, 1024:2048], reads=[r_raw])
                            S.dma("sync", ndv[b, l, t0:t0 + 128, :], raw[:, 2048:3072], reads=[r_raw])
                            S.dma("sync", ngv[b, l, t0:t0 + 128, :], raw[:, 5120:5632], reads=[r_raw])
                            S.dma("sync", nkpe[b, l, t0:t0 + 128, :], raw[:, 7168:7232], reads=[r_raw])
                            S.dma("sync", ngk[b, l, t0:t0 + 128, :], wk[:, 1536:2048], reads=[r_wk])
                            S.dma("sync", nckv[b, l, t0:t0 + 128, :], wk[:, 3072:3584], reads=[r_wk])
                            S.op("vector", lambda e, raw=raw: e.tensor_copy(Qp[:, 0:1024], raw[:, 0:1024]), reads=[r_raw], writes=[r_Qp])
                            S.op("scalar", lambda e: e.activation(out=Qp[:, 1024:2560], in_=wk[:, 0:1536], func=AF.Copy), reads=[r_wk], writes=[r_Qp])
                            S.op("vector", lambda e, raw=raw: e.tensor_copy(Kp[:, 0:1024], raw[:, 1024:2048]), reads=[r_raw], writes=[r_Kp])
                            S.op("scalar", lambda e: e.activation(out=Kp[:, 1024:1536], in_=wk[:, 1536:2048], func=AF.Copy), reads=[r_wk], writes=[r_Kp])
                            S.op("vector", lambda e, raw=raw: e.tensor_copy(Kp[:, 3072:3136], raw[:, 7168:7232]), reads=[r_raw], writes=[r_Kp])
                        else:
                            c64 = cs64[:, 0:32].rearrange("p (a q) -> p a q", a=2); s64 = cs64[:, 32:64].rearrange("p (a q) -> p a q", a=2)
                            c128 = cs128[:, 0:64].rearrange("p (a q) -> p a q", a=2); s128 = cs128[:, 64:128].rearrange("p (a q) -> p a q", a=2)

                            def v5(ap, nvec, qd):
                                return ap.rearrange("p (v a b q) -> p v a b q", v=nvec, a=2, b=2, q=qd)

                            def tv(n, qd):
                                return rtmp[:, :, 0:n * 2 * qd].rearrange("p t (v a q) -> p t v a q", v=n, a=2, q=qd)
                            rope_ops("vector", v5(raw[:, 0:1024], 16, 16), v5(Qp[:, 0:1024], 16, 16), c64, s64, 16, 16,
                                     tv(16, 16), [r_raw, r_cs64], [r_Qp], r_rtmp)
                            rope_ops("gpsimd", v5(raw[:, 1024:2048], 16, 16), v5(Kp[:, 0:1024], 16, 16), c64, s64, 16, 16,
                                     tv(16, 16), [r_raw, r_cs64], [r_Kp], r_rtmp)
                            rope_ops("vector", v5(wk[:, 0:1536], 12, 32), v5(Qp[:, 1024:2560], 12, 32), c128, s128, 12, 32,
                                     tv(12, 32), [r_wk, r_cs128], [r_Qp], r_rtmp)
                            rope_ops("gpsimd", v5(wk[:, 1536:2048], 4, 32), v5(Kp[:, 1024:1536], 4, 32), c128, s128, 4, 32,
                                     tv(4, 32), [r_wk, r_cs128], [r_Kp], r_rtmp)
                            rope_ops("vector", v5(raw[:, 7168:7232], 1, 16), v5(Kp[:, 3072:3136], 1, 16), c64, s64, 1, 16,
                                     tv(1, 16), [r_raw, r_cs64], [r_Kp], r_rtmp)
                        S.op("scalar", lambda e, raw=raw: e.activation(out=Vp[:, 0:1024], in_=raw[:, 2048:3072], func=AF.Copy), reads=[r_raw], writes=[r_Vp])
                        S.op("scalar", lambda e, raw=raw: e.activation(out=Vp[:, 1024:1536], in_=raw[:, 5120:5632], func=AF.Copy), reads=[r_raw], writes=[r_Vp])
                        transposes(cnb, r_cnb, 12, cnT, r_cnT)
                        for nb in range(5):
                            w = 512 if nb < 4 else 256
                            pm, r_pm = pM[nM[0] % 3]
                            nM[0] += 1
                            S.pe([lambda e, pm=pm, kc=kc, nb=nb, w=w: e.matmul(
                                pm[:, 0:w], cnT[:, kc, :], wuq[:, kc, nb * 512:nb * 512 + w],
                                start=(kc == 0), stop=(kc == 7)) for kc in range(8)], reads=[r_cnT, r_wuq], writes=[r_pm])
                            S.op("vector" if nb % 2 else "scalar",
                                 (lambda e, pm=pm, nb=nb, w=w: e.tensor_copy(mqf[:, nb * 512:nb * 512 + w], pm[:, 0:w])) if nb % 2 else
                                 (lambda e, pm=pm, nb=nb, w=w: e.activation(out=mqf[:, nb * 512:nb * 512 + w], in_=pm[:, 0:w], func=AF.Copy)),
                                 reads=[r_pm], writes=[r_mqf])
                        mq3 = mqf[:].rearrange("p (h c) -> p h c", h=12)
                        Qm3 = Qp[:, 2560:4864].rearrange("p (h c) -> p h c", h=12)
                        S.op("scalar", lambda e: e.activation(out=Qm3[:, :, 0:128], in_=mq3[:, :, 0:128], func=AF.Copy), reads=[r_mqf], writes=[r_Qp])
                        if g == 0:
                            S.op("vector", lambda e: e.tensor_copy(Qm3[:, :, 128:192], mq3[:, :, 128:192]), reads=[r_mqf], writes=[r_Qp])
                        else:
                            def v5s(ap3):
                                return ap3.rearrange("p v (a b q) -> p v a b q", a=2, b=2, q=16)
                            rope_ops("vector", v5s(mq3[:, :, 128:192]), v5s(Qm3[:, :, 128:192]), c64, s64, 12, 16,
                                     tv(12, 16), [r_mqf, r_cs64], [r_Qp], r_rtmp)
                        mla_kv(0)
                        transposes(Qp, r_Qp, 38, QTs, r_QTs)
                        for c0, c1 in ((0, 10), (10, 20), (20, 30), (30, 38)):
                            S.dma("sync", QT3[:, c0:c1, tt * 128:(tt + 1) * 128], QTs[:, c0:c1, :], reads=[r_QTs])
                        store_kv(tt * 128 if g == 0 else 1280 + lt * 128)

                with Phase(S) as ph:
                    OT, r_OT = ph.sb("OT", [128, KC, NT], BF16, nres=KC)
                    with Phase(S) as pa:
                        lamt, r_lam = pa.sb("lamt", [128, 8])
                        lv, r_lv = pa.sb("lv", [128, 4, 64])
                        sgn, r_sgn = pa.sb("sgn", [128, 1])
                        for i, v in enumerate((lq1, lk1, lq2, lk2)):
                            S.dma("sync", lv[:, i, :], v[l, :].partition_broadcast(128), writes=[r_lv])
                        S.op("vector", lambda e: e.tensor_tensor(lv[:, 0, :], lv[:, 0, :], lv[:, 1, :], ALU.mult), reads=[r_lv], writes=[r_lv])
                        S.op("vector", lambda e: e.tensor_tensor(lv[:, 2, :], lv[:, 2, :], lv[:, 3, :], ALU.mult), reads=[r_lv], writes=[r_lv])
                        S.op("vector", lambda e: e.tensor_reduce(lamt[:, 0:1], lv[:, 0, :], AX.X, ALU.add), reads=[r_lv], writes=[r_lam])
                        S.op("vector", lambda e: e.tensor_reduce(lamt[:, 1:2], lv[:, 2, :], AX.X, ALU.add), reads=[r_lv], writes=[r_lam])
                        S.op("scalar", lambda e: e.activation(out=lamt[:, 2:4], in_=lamt[:, 0:2], func=AF.Exp), reads=[r_lam], writes=[r_lam])
                        S.op("vector", lambda e: e.scalar_tensor_tensor(lamt[:, 4:5], lamt[:, 3:4], -lam_init(l), lamt[:, 2:3],
                                                                        ALU.add, ALU.subtract), reads=[r_lam], writes=[r_lam])
                        S.dma("sync", sgn[:], subln[l, :].rearrange("(p o) -> p o", o=1), writes=[r_sgn])
                        S.op("vector", lambda e: e.tensor_scalar(sgn[:], sgn[:], 1.0 - lam_init(l), None, ALU.mult), reads=[r_sgn], writes=[r_sgn])
                        neglam = lamt[:, 4:5]

                        NB = 3
                        kts = [pa.sb("kt%d" % i, [128, 1280], BF16) for i in range(NB)]
                        kpes, r_kpes = pa.sb("kpes", [64, 1280], BF16)
                        vts = [pa.sb("vt%d" % i, [128, 10, 128], BF16) for i in range(NB)]
                        qts = [pa.sb("qt%d" % i, [128, 1024], BF16) for i in range(NB)]
                        qpes = [pa.sb("qpe%d" % i, [64, 1024], BF16) for i in range(NB)]
                        ets = [pa.sb("et%d" % i, [128, 512], BF16) for i in range(3)]
                        rcs = [pa.sb("rc%d" % i, [128, 512]) for i in range(2)]
                        o0s, r_o0 = pa.sb("o0s", [128, 512])
                        o1s, r_o1 = pa.sb("o1s", [128, 512])
                        sqb, r_sqb = pa.sb("sqb", [128, 512], BF16)
                        psS = [pa.ps("psS%d" % i, [128, 512]) for i in range(3)]
                        psO = [pa.ps("psO%d" % i, [128, 512]) for i in range(2)]
                        psR = [pa.ps("psR%d" % i, [128, 512]) for i in range(2)]
                        psN, r_psN = pa.ps("psN", [128, 512])
                        cnt = {"s": 0, "o": 0, "e": 0, "k": 0, "v": 0, "q": 0}

                        def attend(qparts, kparts, vt, r_v, nkt, T, scale, finish, rd_extra):
                            for q0 in range(0, T, 512):
                                w = min(512, T - q0)
                                po, r_po = psO[cnt["o"] % 2]
                                pr, r_pr = psR[cnt["o"] % 2]
                                rc, r_rc = rcs[cnt["o"] % 2]
                                cnt["o"] += 1
                                np_ = len(qparts)
                                stage = {}

                                def score(kt):
                                    pS, r_pS = psS[cnt["s"] % 3]
                                    et, r_et = ets[cnt["s"] % 3]
                                    cnt["s"] += 1
                                    S.pe([lambda e, pS=pS, i=i, kt=kt, w=w, q0=q0: e.matmul(
                                        pS[:, 0:w], kparts[i][0][0:kparts[i][1], kt * 128:(kt + 1) * 128],
                                        qparts[i][0][0:qparts[i][1], q0:q0 + w], start=(i == 0), stop=(i == np_ - 1))
                                        for i in range(np_)], reads=rd_extra, writes=[r_pS])
                                    S.op("scalar", lambda e, et=et, pS=pS, w=w: e.activation(
                                        out=et[:, 0:w], in_=pS[:, 0:w], func=AF.Exp, scale=scale), reads=[r_pS], writes=[r_et])
                                    stage[kt] = (et, r_et)

                                def pv(kt):
                                    et, r_et = stage.pop(kt)
                                    S.pe([lambda e, et=et, kt=kt, po=po, w=w: e.matmul(
                                        po[:, 0:w], vt[:, kt, :], et[:, 0:w], start=(kt == 0), stop=(kt == nkt - 1))],
                                        reads=[r_et, r_v], writes=[r_po])
                                    S.pe([lambda e, et=et, kt=kt, pr=pr, w=w: e.matmul(
                                        pr[:, 0:w], onesb[:], et[:, 0:w], start=(kt == 0), stop=(kt == nkt - 1))],
                                        reads=[r_et, r_onesb], writes=[r_pr])
                                LOOK = 2
                                for kt in range(min(LOOK, nkt)):
                                    score(kt)
                                for kt in range(nkt):
                                    if kt + LOOK < nkt:
                                        score(kt + LOOK)
                                    pv(kt)
                                S.op("vector", lambda e, rc=rc, pr=pr, w=w: e.reciprocal(rc[:, 0:w], pr[:, 0:w]), reads=[r_pr], writes=[r_rc])
                                finish(po, r_po, rc, r_rc, q0, w)

                        seqs = [(256 * i, 256, 256 * i, 256) for i in range(4)] + [(1024, 1024, 1024, 1280)]
                        for (tq0, T, k0, Sk) in seqs:
                            nkt = Sk // 128

                            def load_k(row0, d):
                                kt_, r_k = kts[cnt["k"] % NB]
                                cnt["k"] += 1
                                S.dma("sync", kt_[0:d, 0:Sk], KT[row0:row0 + d, k0:k0 + Sk], writes=[r_k])
                                return kt_, r_k

                            def load_v(col0):
                                vt_, r_v = vts[cnt["v"] % NB]
                                cnt["v"] += 1
                                S.dma("sync", vt_[:, 0:nkt, :], VV[k0:k0 + Sk, col0:col0 + 128].rearrange("(kt p) c -> p kt c", p=128),
                                      writes=[r_v])
                                return vt_, r_v

                            def load_q(row0, d):
                                qt_, r_q = qts[cnt["q"] % NB]
                                cnt["q"] += 1
                                S.dma("sync", qt_[0:d, 0:T], QT[row0:row0 + d, tq0:tq0 + T], writes=[r_q])
                                return qt_, r_q

                            def fin_plain(chunk, tq0=tq0):
                                def f(po, r_po, rc, r_rc, q0, w):
                                    S.op("vector", lambda e: e.tensor_tensor(
                                        OT[:, chunk, tq0 + q0:tq0 + q0 + w], po[:, 0:w], rc[:, 0:w], ALU.mult),
                                        reads=[r_po, r_rc], writes=[r_OT[chunk]])
                                return f
                            for h in range(8):
                                kt_, r_k = load_k(h * 128, 128)
                                qt_, r_q = load_q(h * 128, 128)
                                vt_, r_v = load_v(h * 128)

                                def fin0(po, r_po, rc, r_rc, q0, w):
                                    S.op("vector", lambda e: e.tensor_tensor(o0s[:, q0 % 512:q0 % 512 + w], po[:, 0:w], rc[:, 0:w], ALU.mult),
                                         reads=[r_po, r_rc], writes=[r_o0])
                                for q0 in range(0, T, 512):
                                    w = min(512, T - q0)

                                    def run_map(c, fin):
                                        qs = qt_[c * 64:(c + 1) * 64, q0:q0 + w]
                                        ks = kt_[c * 64:(c + 1) * 64, :]
                                        attend([(qs, 64)], [(ks, 64)], vt_, r_v, nkt, w, 64 ** -0.5,
                                               lambda po, r_po, rc, r_rc, _q, _w: fin(po, r_po, rc, r_rc, 0, _w), [r_k, r_q])
                                    run_map(0, fin0)

                                    def fin1(po, r_po, rc, r_rc, _q, w, h=h, q0=q0, tq0=tq0):
                                        S.op("vector", lambda e: e.tensor_tensor(o1s[:, 0:w], po[:, 0:w], rc[:, 0:w], ALU.mult),
                                             reads=[r_po, r_rc], writes=[r_o1])
                                        S.op("vector", lambda e: e.scalar_tensor_tensor(
                                            o0s[:, 0:w], o1s[:, 0:w], neglam, o0s[:, 0:w], ALU.mult, ALU.add),
                                            reads=[r_o0, r_o1, r_lam], writes=[r_o0])
                                        S.op("scalar", lambda e: e.activation(out=sqb[:, 0:w], in_=o0s[:, 0:w], func=AF.Square),
                                             reads=[r_o0], writes=[r_sqb])
                                        S.pe([lambda e: e.matmul(psN[:, 0:w], onesb[:], sqb[:, 0:w], start=True, stop=True)],
                                             reads=[r_sqb, r_onesb], writes=[r_psN])
                                        S.op("scalar", lambda e: e.activation(out=o1s[:, 0:w], in_=psN[:, 0:w], func=AF.Sqrt,
                                                                              bias=epsc[:, 0:1], scale=1.0 / 128),
                                             reads=[r_psN, r_epsc], writes=[r_o1])
                                        S.op("vector", lambda e: e.reciprocal(o1s[:, 0:w], o1s[:, 0:w]), reads=[r_o1], writes=[r_o1])
                                        S.op("vector", lambda e: e.scalar_tensor_tensor(
                                            OT[:, h, tq0 + q0:tq0 + q0 + w], o0s[:, 0:w], sgn[:, 0:1], o1s[:, 0:w], ALU.mult, ALU.mult),
                                            reads=[r_o0, r_o1, r_sgn], writes=[r_OT[h]])
                                    run_map(1, fin1)
                            for kvh in range(4):
                                kt_, r_k = load_k(1024 + kvh * 128, 128)
                                vt_, r_v = load_v(1024 + kvh * 128)
                                for gi in range(3):
                                    hq = kvh * 3 + gi
                                    qt_, r_q = load_q(1024 + hq * 128, 128)
                                    attend([(qt_, 128)], [(kt_, 128)], vt_, r_v, nkt, T, 128 ** -0.5, fin_plain(8 + hq), [r_k, r_q])
                            S.dma("sync", kpes[:, 0:Sk], KT[3072:3136, k0:k0 + Sk], writes=[r_kpes])
                            for h in range(12):
                                kt_, r_k = load_k(1536 + h * 128, 128)
                                vt_, r_v = load_v(1536 + h * 128)
                                qt_, r_q = load_q(2560 + h * 192, 128)
                                qp_, r_qp = qpes[h % NB]
                                S.dma("sync", qp_[:, 0:T], QT[2560 + h * 192 + 128:2560 + h * 192 + 192, tq0:tq0 + T], writes=[r_qp])
                                attend([(qt_, 128), (qp_, 64)], [(kt_, 128), (kpes, 64)], vt_, r_v, nkt, T, 192 ** -0.5,
                                       fin_plain(20 + h), [r_k, r_q, r_qp, r_kpes])
                    wov = w_out[l].rearrange("(kc p) n -> p kc n", p=128)
                    wps = [ph.sb("owp%d" % i, [128, KC, 256], BF16) for i in range(2)]
                    pss = [ph.ps("ops%d" % i, [128, 512]) for i in range(4)]
                    xbs = [ph.sb("oxb%d" % i, [128, 512]) for i in range(4)]
                    n = 0
                    for pn in range(D // 256):
                        wp, r_wp = wps[pn % 2]
                        for s4 in range(4):
                            S.dma("gpsimd", wp[:, s4 * 8:(s4 + 1) * 8, :], wov[:, s4 * 8:(s4 + 1) * 8, pn * 256:(pn + 1) * 256], writes=[r_wp])
                        for db in range(2):
                            ch = pn * 2 + db
                            for tb in range(4):
                                g = tb // 2
                                pst, r_ps = pss[n % 4]
                                xb, r_xb = xbs[n % 4]
                                n += 1
                                S.dma("sync", xb[:], XT3[:, ch, tb * 512:(tb + 1) * 512], writes=[r_xb])
                                S.pe([lambda e, pst=pst, wp=wp, kc=kc, db=db, tb=tb: e.matmul(
                                    pst[:], wp[:, kc, db * 128:(db + 1) * 128], OT[:, kc, tb * 512:(tb + 1) * 512],
                                    start=(kc == 0), stop=(kc == KC - 1)) for kc in range(KC)],
                                    reads=[r_wp] + r_OT, writes=[r_ps])
                                S.op("vector", lambda e, xb=xb, pst=pst, ch=ch, g=g: e.scalar_tensor_tensor(
                                    xb[:], pst[:], gT[:, 1, ch, g:g + 1], xb[:], ALU.mult, ALU.add),
                                    reads=[r_ps, r_xb, r_gT], writes=[r_xb])
                                S.dma("sync", XT3[:, ch, tb * 512:(tb + 1) * 512], xb[:], reads=[r_xb])

            stop = debug or "all"
            done = False
            mod_setup(0)
            with Phase(S) as ph0:
                for _ in mod_gen(ph0, 0, 0, 3 * KC, [0]):
                    pass
            for l in range(depth):
                phase_ffn(l, 0, hosted=(l, 3 * KC, NMOD * KC, [1, 2]))
                if stop == "ffn1_%d" % l:
                    done = True
                    break
                phase_mixer(l)
                if stop == "mix_%d" % l:
                    done = True
                    break
                if l + 1 < depth:
                    mod_setup(l + 1)
                    phase_ffn(l, 1, hosted=(l + 1, 0, 3 * KC, [0]))
                else:
                    phase_ffn(l, 1)

            if debug:
                with Phase(S) as ph:
                    bufs = [ph.sb("dbb%d" % i, [128, KC, 128]) for i in range(2)]
                    dbg3 = dbg.rearrange("(kc p) t -> p kc t", p=128)
                    for tt in range(NTILE):
                        b, r_b = bufs[tt % 2]
                        S.dma("sync", b[:], XT3[:, :, tt * 128:(tt + 1) * 128], writes=[r_b])
                        S.dma("sync", dbg3[:, :, tt * 128:(tt + 1) * 128], b[:], reads=[r_b])

            with Phase(S) as ph:
                fnT, r_fnT = ph.sb("fnT", [128, KC])
                psA, r_psA = ph.ps("fpsA", [128, 512])
                load_vec_T(ph, final_norm, KC, fnT, r_fnT, psA, r_psA)
                xts = [ph.sb("fxt%d" % i, [128, KC, 128]) for i in range(2)]
                sqs = [ph.sb("fsq%d" % i, [128, KC, 128], BF16) for i in range(2)]
                rss = [ph.sb("frs%d" % i, [128, 128]) for i in range(2)]
                yts = [ph.sb("fyt%d" % i, [128, D]) for i in range(2)]
                pss = [ph.ps("fps%d" % i, [128, 512]) for i in range(2)]
                pst4 = [ph.ps("fpt%d" % i, [128, 512]) for i in range(4)]
                for tt in range(NTILE):
                    g, lt = tt // 8, tt % 8
                    xt, r_xt = xts[tt % 2]; sq, r_sq = sqs[tt % 2]; rs, r_rs = rss[tt % 2]
                    yt, r_yt = yts[tt % 2]; pst, r_ps = pss[tt % 2]
                    S.dma("sync", xt[:], XT3[:, :, tt * 128:(tt + 1) * 128], writes=[r_xt])
                    S.op("scalar", lambda e, sq=sq, xt=xt: e.activation(out=sq[:], in_=xt[:], func=AF.Square), reads=[r_xt], writes=[r_sq])
                    S.pe([lambda e, pst=pst, sq=sq, kc=kc: e.matmul(pst[:, 0:128], onesb[:], sq[:, kc, :],
                                                                    start=(kc == 0), stop=(kc == KC - 1)) for kc in range(KC)],
                         reads=[r_sq, r_onesb], writes=[r_ps])
                    S.op("scalar", lambda e, rs=rs, pst=pst: e.activation(out=rs[:], in_=pst[:, 0:128], func=AF.Sqrt,
                                                                          bias=epsc[:, 0:1], scale=1.0 / D),
                         reads=[r_ps, r_epsc], writes=[r_rs])
                    S.op("vector", lambda e, rs=rs: e.reciprocal(rs[:], rs[:]), reads=[r_rs], writes=[r_rs])
                    S.op("vector", lambda e, xt=xt, rs=rs: e.tensor_tensor(
                        xt[:], xt[:], rs[:].rearrange("p (o t) -> p o t", o=1).broadcast_to([128, KC, 128]), ALU.mult),
                        reads=[r_xt, r_rs], writes=[r_xt])
                    S.op("gpsimd", lambda e, xt=xt: e.tensor_tensor(
                        xt[:], xt[:], fnT[:].rearrange("p (k o) -> p k o", o=1).broadcast_to([128, KC, 128]), ALU.mult),
                        reads=[r_xt, r_fnT], writes=[r_xt])
                    for k4 in range(8):
                        p4, r_p4 = pst4[k4 % 4]
                        S.pe([lambda e, p4=p4, xt=xt, i=i, kc=k4 * 4 + i: e.matmul(
                            p4[:, i * 128:(i + 1) * 128], xt[:, kc, :], ident[:], start=True, stop=True) for i in range(4)],
                            reads=[r_xt, r_ident], writes=[r_p4])
                        if k4 % 2 == 0:
                            S.op("vector", lambda e, yt=yt, p4=p4, k4=k4: e.tensor_copy(yt[:, k4 * 512:(k4 + 1) * 512], p4[:]),
                                 reads=[r_p4], writes=[r_yt])
                        else:
                            S.op("scalar", lambda e, yt=yt, p4=p4, k4=k4: e.activation(out=yt[:, k4 * 512:(k4 + 1) * 512], in_=p4[:], func=AF.Copy),
                                 reads=[r_p4], writes=[r_yt])
                    S.dma("sync", yout[g][lt * 128:(lt + 1) * 128, :], yt[:], reads=[r_yt])
        S.emit()
    return nc


def rope_tables():
    T = 1024
    t = np.arange(T)
    rows = (t // 64).astype(np.float32)
    cols = (t % 64).astype(np.float32)
    out = {}
    for dh in (64, 128):
        half = dh // 2
        inv = (np.float32(10000.0) ** (-(np.arange(0, half, 2, dtype=np.float32) / np.float32(half)))).astype(np.float32)
        ar = (rows[:, None] * inv[None, :]).astype(np.float32)
        ac = (cols[:, None] * inv[None, :]).astype(np.float32)
        tab = np.concatenate([np.cos(ar), np.cos(ac), np.sin(ar), np.sin(ac)], axis=1).astype(np.float32)
        out[dh] = np.ascontiguousarray(tab)
    return out


_CACHE = {}


def make_in_maps(inputs, debug=None):
    f = lambda a: np.ascontiguousarray(np.asarray(a, dtype=np.float32))
    tabs = rope_tables()
    shared = {k: f(inputs[k]) for k in (
        "w_mod", "b_mod", "norm_ffn1", "norm_mix", "norm_ffn2", "ffn1_w_in", "ffn1_w_out", "ffn2_w_in", "ffn2_w_out",
        "w_in", "w_out", "diff_lq1", "diff_lk1", "diff_lq2", "diff_lk2", "diff_subln", "gqa_q_norm", "gqa_k_norm",
        "mla_q_norm", "mla_kv_norm", "mla_w_uq", "mla_w_ukv", "final_norm")}
    shared["ident"] = np.eye(128, dtype=np.float32)
    shared["cs64"] = tabs[64]
    shared["cs128"] = tabs[128]
    xp = f(inputs["x_prompt"]); xs = f(inputs["x_sample"])
    c = f(inputs["c"]); c_ctx = f(inputs["c_ctx"])
    maps = []
    for i in range(NCORES):
        m = dict(shared)
        m["xp"] = xp[4 * i:4 * i + 4].reshape(1024, D)
        m["xs"] = xs[i].reshape(1024, D)
        m["cdk"] = f(inputs["cache_diff_k"][i]).reshape(2, 256, 1024)
        m["cdv"] = f(inputs["cache_diff_v"][i]).reshape(2, 256, 1024)
        m["cgk"] = f(inputs["cache_gqa_k"][i]).reshape(2, 256, 512)
        m["cgv"] = f(inputs["cache_gqa_v"][i]).reshape(2, 256, 512)
        m["cckv"] = f(inputs["cache_mla_ckv"][i]).reshape(2, 256, 512)
        m["ckpe"] = f(inputs["cache_mla_kpe"][i]).reshape(2, 256, 64)
        m["cvec"] = np.stack([c_ctx, c[i]], axis=0)
        maps.append(m)
    return maps


def kernel(**inputs):
    if "nc" not in _CACHE:
        _CACHE["nc"] = build_program()
    nc = _CACHE["nc"]
    maps = make_in_maps(inputs)
    res = run_bass_kernel_spmd(nc, maps, core_ids=list(range(NCORES)))
    R = res.results
    y_prompt = np.concatenate([R[i]["yp"].reshape(4, 256, D) for i in range(NCORES)], axis=0)
    y_sample = np.stack([R[i]["ys"].reshape(1024, D) for i in range(NCORES)], axis=0)
    ndk = np.concatenate([R[i]["ndk"].reshape(4, 2, 256, 8, 2, 64) for i in range(NCORES)], axis=0)
    ndv = np.concatenate([R[i]["ndv"].reshape(4, 2, 256, 8, 128) for i in range(NCORES)], axis=0)
    ngk = np.concatenate([R[i]["ngk"].reshape(4, 2, 256, 4, 128) for i in range(NCORES)], axis=0)
    ngv = np.concatenate([R[i]["ngv"].reshape(4, 2, 256, 4, 128) for i in range(NCORES)], axis=0)
    nckv = np.concatenate([R[i]["nckv"].reshape(4, 2, 256, 512) for i in range(NCORES)], axis=0)
    nkpe = np.concatenate([R[i]["nkpe"].reshape(4, 2, 256, 64) for i in range(NCORES)], axis=0)
    return (y_prompt.astype(np.float32), y_sample.astype(np.float32), ndk, ndv, ngk, ngv, nckv, nkpe)
```
